# Optimizing a Trainium2 kernel written in Bass

```python
import math
import jax, jax.numpy as jnp
from jax import lax
import numpy as np

D_MODEL = 1024
BATCH = 8
SEQ = 4096
DEPTH = 1

D_RNN = 1024
RNN_BLOCKS = 16
RNN_BLOCK_W = D_RNN // RNN_BLOCKS
CONV_W = 4
LRU_C = 8.0
N_Q_HEADS = 16
N_KV_HEADS = 4
GROUP = N_Q_HEADS // N_KV_HEADS
HEAD_DIM = 64
D_ATTN = N_Q_HEADS * HEAD_DIM
D_KV = N_KV_HEADS * HEAD_DIM
WINDOW = 128
BLOCK = 128
ALIBI_MAX_BIAS = 8.0
N_BRANCH = 2
EPS = 1e-6

SPLIT_SIZES = (D_RNN, D_RNN, D_ATTN, D_KV, D_KV, D_ATTN, N_BRANCH * D_MODEL)
D_IN = sum(SPLIT_SIZES)
SPLIT_POINTS = tuple(int(v) for v in np.cumsum(SPLIT_SIZES)[:-1])

kernel_name = "hybrid_rglru_swa_sink_alibi_block"


def rms_norm(x, g):
    xf = x.astype(jnp.float32)
    y = xf * lax.rsqrt(jnp.mean(xf * xf, axis=-1, keepdims=True) + EPS)
    return (y * g.astype(jnp.float32)).astype(x.dtype)


def causal_depthwise_conv(x, w, b):
    y = lax.conv_general_dilated(
        x, w[:, None, :].astype(x.dtype), window_strides=(1,),
        padding=[(CONV_W - 1, 0)], dimension_numbers=("NWC", "WIO", "NWC"),
        feature_group_count=x.shape[-1])
    return y + b


def rg_lru(x, w_a, b_a, w_x, b_x, lam):
    B, T, _ = x.shape
    xb = x.reshape(B, T, RNN_BLOCKS, RNN_BLOCK_W)
    r = jax.nn.sigmoid(jnp.einsum("bthi,hij->bthj", xb, w_a) + b_a).reshape(B, T, D_RNN)
    i = jax.nn.sigmoid(jnp.einsum("bthi,hij->bthj", xb, w_x) + b_x).reshape(B, T, D_RNN)
    log_a = -LRU_C * r.astype(jnp.float32) * jax.nn.softplus(-lam.astype(jnp.float32))
    a = jnp.exp(log_a)
    mult = jnp.sqrt(-jnp.expm1(2.0 * log_a))
    u = mult * (i * x).astype(jnp.float32)

    def combine(left, right):
        a_l, u_l = left
        a_r, u_r = right
        return a_l * a_r, a_r * u_l + u_r

    _, h = lax.associative_scan(combine, (a, u), axis=1)
    return h.astype(x.dtype)


def sliding_window_sink_attention(q, k, v, sinks):
    B, T, _, _ = q.shape
    nb = T // BLOCK
    scale = HEAD_DIM ** -0.5
    qb = q.reshape(B, nb, BLOCK, N_KV_HEADS, GROUP, HEAD_DIM)

    def band(t):
        tp = jnp.pad(t, ((0, 0), (BLOCK, 0), (0, 0), (0, 0)))
        tp = tp.reshape(B, nb + 1, BLOCK, N_KV_HEADS, HEAD_DIM)
        return jnp.concatenate([tp[:, :-1], tp[:, 1:]], axis=2)

    kw, vw = band(k), band(v)
    scores = jnp.einsum("bnqhgd,bnkhd->bnhgqk", qb, kw).astype(jnp.float32) * scale

    q_loc = jnp.arange(BLOCK)[:, None] + BLOCK
    k_loc = jnp.arange(2 * BLOCK)[None, :]
    dist = q_loc - k_loc
    in_window = (dist >= 0) & (dist < WINDOW)
    k_abs = jnp.arange(nb)[:, None] * BLOCK - BLOCK + jnp.arange(2 * BLOCK)[None, :]
    mask = in_window[None, :, :] & (k_abs >= 0)[:, None, :]

    slopes = 2.0 ** (-ALIBI_MAX_BIAS * jnp.arange(1, N_Q_HEADS + 1, dtype=jnp.float32) / N_Q_HEADS)
    slopes = slopes.reshape(N_KV_HEADS, GROUP)
    alibi = -slopes[:, :, None, None] * dist.astype(jnp.float32)

    scores = jnp.where(mask[None, :, None, None], scores + alibi[None, None], jnp.float32(-1e30))
    sink = sinks.astype(jnp.float32).reshape(N_KV_HEADS, GROUP)[None, None, :, :, None, None]
    m = jnp.maximum(jnp.max(scores, axis=-1, keepdims=True), sink)
    p = jnp.exp(scores - m)
    denom = jnp.sum(p, axis=-1, keepdims=True) + jnp.exp(sink - m)
    probs = (p / denom).astype(v.dtype)
    out = jnp.einsum("bnhgqk,bnkhd->bnqhgd", probs, vw)
    return out.reshape(B, T, D_ATTN)


def setup_inputs(seed: int = 0) -> dict:
    key = jax.random.key(seed)
    ks = jax.random.split(key, 16)
    f32 = jnp.float32
    x = jax.random.normal(ks[0], (BATCH, SEQ, D_MODEL), f32)
    pre_norm_g = 1.0 + 0.05 * jax.random.normal(ks[1], (DEPTH, D_MODEL), f32)
    w_in = jax.random.normal(ks[2], (DEPTH, D_MODEL, D_IN), f32) * D_MODEL ** -0.5
    b_gate = 0.01 * jax.random.normal(ks[3], (DEPTH, N_BRANCH * D_MODEL), f32)
    conv_w = jax.random.normal(ks[4], (DEPTH, CONV_W, D_RNN), f32) * CONV_W ** -0.5
    conv_b = 0.01 * jax.random.normal(ks[5], (DEPTH, D_RNN), f32)
    w_rg_a = jax.random.normal(ks[6], (DEPTH, RNN_BLOCKS, RNN_BLOCK_W, RNN_BLOCK_W), f32) * RNN_BLOCK_W ** -0.5
    b_rg_a = 0.01 * jax.random.normal(ks[7], (DEPTH, RNN_BLOCKS, RNN_BLOCK_W), f32)
    w_rg_x = jax.random.normal(ks[8], (DEPTH, RNN_BLOCKS, RNN_BLOCK_W, RNN_BLOCK_W), f32) * RNN_BLOCK_W ** -0.5
    b_rg_x = 0.01 * jax.random.normal(ks[9], (DEPTH, RNN_BLOCKS, RNN_BLOCK_W), f32)
    a0 = jax.random.uniform(ks[10], (DEPTH, D_RNN), f32, minval=0.9, maxval=0.999)
    a_base = a0 ** (1.0 / LRU_C)
    lru_lambda = jnp.log(a_base) - jnp.log1p(-a_base)
    attn_sinks = 0.5 * jax.random.normal(ks[11], (DEPTH, N_Q_HEADS), f32)
    w_rnn_out = jax.random.normal(ks[12], (DEPTH, D_RNN, D_MODEL), f32) * D_RNN ** -0.5
    w_attn_out = jax.random.normal(ks[13], (DEPTH, D_ATTN, D_MODEL), f32) * D_ATTN ** -0.5
    w_out = jax.random.normal(ks[14], (DEPTH, D_MODEL, D_MODEL), f32) * D_MODEL ** -0.5
    post_norm_g = 1.0 + 0.05 * jax.random.normal(ks[15], (DEPTH, D_MODEL), f32)
    return {"x": x, "pre_norm_g": pre_norm_g, "w_in": w_in, "b_gate": b_gate,
            "conv_w": conv_w, "conv_b": conv_b, "w_rg_a": w_rg_a, "b_rg_a": b_rg_a,
            "w_rg_x": w_rg_x, "b_rg_x": b_rg_x, "lru_lambda": lru_lambda,
            "attn_sinks": attn_sinks, "w_rnn_out": w_rnn_out, "w_attn_out": w_attn_out,
            "w_out": w_out, "post_norm_g": post_norm_g}


def reference(x, pre_norm_g, w_in, b_gate, conv_w, conv_b, w_rg_a, b_rg_a, w_rg_x, b_rg_x,
              lru_lambda, attn_sinks, w_rnn_out, w_attn_out, w_out, post_norm_g):
    B, T, _ = x.shape
    for l in range(DEPTH):
        h = rms_norm(x, pre_norm_g[l])
        proj = h @ w_in[l]
        rnn_x, rnn_gate, q, k, v, attn_gate, merge_logits = jnp.split(proj, SPLIT_POINTS, axis=-1)

        c = causal_depthwise_conv(rnn_x, conv_w[l], conv_b[l])
        y_rnn = rg_lru(c, w_rg_a[l], b_rg_a[l], w_rg_x[l], b_rg_x[l], lru_lambda[l])
        br_rnn = (y_rnn * jax.nn.silu(rnn_gate)) @ w_rnn_out[l]

        qh = q.reshape(B, T, N_Q_HEADS, HEAD_DIM)
        kh = k.reshape(B, T, N_KV_HEADS, HEAD_DIM)
        vh = v.reshape(B, T, N_KV_HEADS, HEAD_DIM)
        y_attn = sliding_window_sink_attention(qh, kh, vh, attn_sinks[l])
        br_attn = (y_attn * jax.nn.silu(attn_gate)) @ w_attn_out[l]

        g_rnn, g_attn = jnp.split(jax.nn.sigmoid(merge_logits + b_gate[l]), N_BRANCH, axis=-1)
        merged = g_rnn * br_rnn + g_attn * br_attn
        out = merged @ w_out[l]
        x = x + rms_norm(out, post_norm_g[l])
    return x
```

```python
import numpy as np
from contextlib import ExitStack
import concourse.bass as bass
import concourse.mybir as mybir
from concourse.bass_utils import run_bass_kernel_spmd

F32 = mybir.dt.float32
BF16 = mybir.dt.bfloat16
AF = mybir.ActivationFunctionType
ALU = mybir.AluOpType

D = 1024
T = 4096
TC = 512
NCH = T // TC
EPS = 1e-6
N_CORES = 8
ENGINES = ("pe", "act", "dve", "pool", "sp")


class Op:
    __slots__ = ("idx", "eng", "fn", "deps", "dma", "signal", "semid", "semval", "tag")

    def __init__(self, idx, eng, fn, dma):
        self.idx = idx
        self.eng = eng
        self.fn = fn
        self.dma = dma
        self.deps = set()
        self.signal = False
        self.semid = None
        self.semval = 0


class Sched:
    N_DMA_SEMS = 24
    count_pe = False

    def __init__(self, nc):
        self.nc = nc
        self.ops = []
        self.last_writer = {}
        self.readers = {}
        self.dma_count = 0
        self.dma_last_on_sem = {}
        self.pe_map = []

    stage = ""

    def op(self, eng, fn, reads=(), writes=(), dma=False, own_sem=False):
        o = Op(len(self.ops), eng, fn, dma)
        o.tag = self.stage
        for t in reads:
            w = self.last_writer.get(t)
            if w is not None:
                o.deps.add(w)
        for t in writes:
            w = self.last_writer.get(t)
            if w is not None:
                o.deps.add(w)
            for r in self.readers.get(t, ()):
                o.deps.add(r)
        for t in writes:
            self.last_writer[t] = o.idx
            self.readers[t] = []
        for t in reads:
            if t not in writes:
                self.readers.setdefault(t, []).append(o.idx)
        if dma and own_sem:
            self.n_own = getattr(self, "n_own", 0) + 1
            o.semid = ("own", self.n_own)
        elif dma:
            k = self.dma_count % self.N_DMA_SEMS
            self.dma_count += 1
            prev = self.dma_last_on_sem.get(k)
            if prev is not None:
                o.deps.add(prev)
            self.dma_last_on_sem[k] = o.idx
            o.semid = ("dma", k)
        else:
            o.semid = ("eng", eng)
        o.deps.discard(o.idx)
        self.ops.append(o)
        return o

    def emit(self):
        nc = self.nc
        ops = self.ops
        for o in ops:
            for d in o.deps:
                p = ops[d]
                if p.dma:
                    p.signal = True
                elif p.eng == "pe" and o.eng == "pe" and not o.dma:
                    continue
                else:
                    p.signal = True
        counters = {}
        for o in ops:
            if o.dma:
                o.signal = True
            if o.signal:
                inc = 16 if o.dma else 1
                counters[o.semid] = counters.get(o.semid, 0) + inc
                o.semval = counters[o.semid]
        semids = sorted(counters.keys(), key=str)
        with ExitStack() as es:
            sems = {}
            for sid in semids:
                sems[sid] = es.enter_context(nc.semaphore("s_%s_%s" % sid))
            block = es.enter_context(nc.Block())
            per_eng = {e: [o for o in ops if o.eng == e] for e in ENGINES}

            def run(e, handle):
                seen = {}
                cnt = [0]

                class _P:
                    def matmul(self_, *a, **k):
                        cnt[0] += 1
                        return handle.matmul(*a, **k)

                    def transpose(self_, *a, **k):
                        cnt[0] += 1
                        return handle.transpose(*a, **k)
                proxy = _P()
                for o in per_eng[e]:
                    need = {}
                    for d in o.deps:
                        p = ops[d]
                        if not p.signal:
                            continue
                        if (not p.dma) and p.eng == "pe" and e == "pe" and not o.dma:
                            continue
                        if p.semval > need.get(p.semid, 0):
                            need[p.semid] = p.semval
                    for sid, v in need.items():
                        if seen.get(sid, 0) >= v:
                            continue
                        seen[sid] = v
                        handle.wait_ge(sems[sid], v)
                    if e == "pe" and self.count_pe:
                        n0 = cnt[0]
                        ins = o.fn(proxy)
                        self.pe_map.append((o.tag, n0, cnt[0]))
                    else:
                        ins = o.fn(handle)
                    if o.signal:
                        ins.then_inc(sems[o.semid], 16 if o.dma else 1)

            if per_eng["pe"]:
                @block.tensor
                def _(h):
                    run("pe", h)
            if per_eng["act"]:
                @block.scalar
                def _(h):
                    run("act", h)
            if per_eng["dve"]:
                @block.vector
                def _(h):
                    run("dve", h)
            if per_eng["pool"]:
                @block.gpsimd
                def _(h):
                    run("pool", h)
            if per_eng["sp"]:
                @block.sync
                def _(h):
                    run("sp", h)


PV_CW = 0
PV_CB = 32
PV_BA = 40
PV_BX = 48
PV_LAM = 56
PV_SINK = 64
PV_BG = 72
PV_N = 88

SLOPES = [2.0 ** (-8.0 * (i + 1) / 16.0) for i in range(16)]
EW = "pool"


def build_program(nch=NCH, last_stages="FRAMO", ndma=24, debug=False):
    nc = bass.Bass("TRN2", target_bir_lowering=False)
    x = nc.dram_tensor("x", [T, D], F32, kind="ExternalInput").ap()
    wR = nc.dram_tensor("wR", [8, 128, 8, 256], F32, kind="ExternalInput").ap()
    wA = nc.dram_tensor("wA", [4, 128, 8, 704], F32, kind="ExternalInput").ap()
    wM = nc.dram_tensor("wM", [8, 128, 8, 256], F32, kind="ExternalInput").ap()
    wO3 = nc.dram_tensor("wO3", [3, 128, 8, 1024], F32, kind="ExternalInput").ap()
    wG = nc.dram_tensor("wG", [2, 128, 8, 128], F32, kind="ExternalInput").ap()
    pvec = nc.dram_tensor("pvec", [128, PV_N], F32, kind="ExternalInput").ap()
    gbc = nc.dram_tensor("gbc", [2, 128, 1024], F32, kind="ExternalInput").ap()
    dmask_d = nc.dram_tensor("dmask", [128, 256], F32, kind="ExternalInput").ap()
    out = nc.dram_tensor("out", [T, D], F32, kind="ExternalOutput").ap()

    if debug:
        dbg_mrg = nc.dram_tensor("dbg_mrg", [128, 8 * TC], BF16, kind="ExternalOutput").ap()
        dbg_ot = nc.dram_tensor("dbg_ot", [128, 1024], F32, kind="ExternalOutput").ap()
        dbg_sso = nc.dram_tensor("dbg_sso", [128, 16], F32, kind="ExternalOutput").ap()
        dbg_xt = nc.dram_tensor("dbg_xt", [128, 1024], F32, kind="ExternalOutput").ap()
        dbg_wo = nc.dram_tensor("dbg_wo", [128, 8 * 1024], BF16, kind="ExternalOutput").ap()
        dbg_hT = nc.dram_tensor("dbg_hT", [128, 8 * TC], BF16, kind="ExternalOutput").ap()
        dbg_ygr = nc.dram_tensor("dbg_ygr", [128, 8 * TC], BF16, kind="ExternalOutput").ap()
        dbg_yga = nc.dram_tensor("dbg_yga", [128, 8 * TC], BF16, kind="ExternalOutput").ap()
    S = Sched(nc)
    S.N_DMA_SEMS = ndma
    sb = nc.alloc_sbuf_tensor

    wro = sb("wro", [128, 8, 1024], BF16)
    wao = sb("wao", [128, 8, 1024], BF16)
    wo = sb("wo", [128, 8, 1024], BF16)
    wga = sb("wga", [128, 8, 128], BF16)
    wgx = sb("wgx", [128, 8, 128], BF16)
    gpre = sb("gpre", [128, 1024], F32)
    gpost = sb("gpost", [128, 1024], F32)
    dmask = sb("dmask_sb", [128, 256], F32)
    pv = sb("pv", [128, PV_N], F32)
    dc = sb("dc", [128, 64], F32)
    DC_HBA, DC_HBX, DC_S, DC_HS, DC_ESINK, DC_HBG = 0, 8, 16, 24, 32, 40
    ident = sb("ident", [128, 128], BF16)
    identf = sb("identf", [128, 128], F32)
    ones = sb("ones", [128, 64], BF16)
    wslot = [sb("wslot%d" % i, [128, 8 * 448], BF16) for i in range(2)]

    def wv(i, ncols):
        return wslot[i][:, 0:8 * ncols].rearrange("p (d c) -> p d c", d=8)
    xt = [sb("xt%d" % i, [128, 1024], F32) for i in range(2)]
    hTs = [sb("hT%d" % i, [128, 8, TC], BF16) for i in range(2)]
    ygrs = [sb("ygr%d" % i, [128, 8, TC], BF16) for i in range(2)]
    Mt = sb("Mt", [128, 4, TC], F32)
    CH = {"p": 0}
    yga = sb("yga", [128, 8, TC], BF16)
    mrg = sb("mrg", [128, 8, TC], BF16)
    qT = sb("qT", [128, 2, TC], BF16)
    kbuf = [sb("kbuf%d" % g, [128, 128 + TC], BF16) for g in range(4)]
    vbuf = [sb("vbuf%d" % g, [128, 5, 64], BF16) for g in range(4)]
    cstate = sb("cstate", [128, 8, 4], F32)
    hstate = sb("hstate", [128, 8], F32)
    ssf = sb("ssf", [128, 8], F32)
    rstf = sb("rstf", [128, 8], F32)
    sso = sb("sso", [128, 16], F32)
    NG = 26
    G = sb("G", [128, NG, 512], F32)
    xr = [sb("xr%d" % i, [128, 3 + TC], F32) for i in range(2)]

    def g32(i):
        return G[:, i, :]

    def gbf(i):
        return G[:, i, :].bitcast(BF16)

    HB0 = 13

    def hbv(tt):
        return gbf(HB0 + tt)

    PS2 = [nc.alloc_psum_tensor("ps2_%d" % i, [128, 1024], F32) for i in range(4)]

    def bank(b):
        return PS2[b // 2][:, (b % 2) * 512:(b % 2 + 1) * 512]

    def btok(b):
        return ("bank", b)

    dma_w = "pool"

    S.op("sp", lambda e: e.dma_start(out=pv[:], in_=pvec), writes=["pv"], dma=True)
    S.op("sp", lambda e: e.dma_start(out=gpre[:], in_=gbc[0]), writes=["gpre"], dma=True)
    S.op("sp", lambda e: e.dma_start(out=dmask[:], in_=dmask_d), writes=["dmask"], dma=True)
    scrR = nc.dram_tensor("scrR", [8, 128, 8, 256], BF16, kind="Internal").ap()
    scrAq = nc.dram_tensor("scrAq", [4, 128, 8, 448], BF16, kind="Internal").ap()
    scrAg = nc.dram_tensor("scrAg", [4, 128, 8, 256], BF16, kind="Internal").ap()
    scrM = nc.dram_tensor("scrM", [8, 128, 8, 256], BF16, kind="Internal").ap()
    S.op("dve", lambda e: e.memset(identf[:], 0.0), writes=["identf"])
    S.op("pool", lambda e: e.affine_select(out=identf[:], in_=identf[:], pattern=[[-1, 128]],
                                           compare_op=ALU.not_equal, fill=1.0, base=0,
                                           channel_multiplier=1),
         reads=["identf"], writes=["identf"])
    S.op("dve", lambda e: e.tensor_copy(out=ident[:], in_=identf[:]), reads=["identf"], writes=["ident"])
    S.op(dma_w, lambda e: e.dma_start(out=wv(0, 256), in_=wR[0]), writes=[("wslot", 0)], dma=True,
         own_sem=True)
    S.op(dma_w, lambda e: e.dma_start(out=wv(1, 256), in_=wR[1]), writes=[("wslot", 1)], dma=True,
         own_sem=True)
    S.op(dma_w, lambda e: e.dma_start(out=scrR[0:2], in_=wR[0:2]), writes=["scrR0"], dma=True, own_sem=True)
    S.op(dma_w, lambda e: e.dma_start(out=wga[:], in_=wG[0]), writes=["wga"], dma=True, own_sem=True)
    S.op(dma_w, lambda e: e.dma_start(out=wgx[:], in_=wG[1]), writes=["wgx"], dma=True, own_sem=True)

    def late_casts_1():
        S.op(dma_w, lambda e: e.dma_start(out=scrR[2:8], in_=wR[2:8]), writes=["scrR1"], dma=True, own_sem=True)
        S.op(dma_w, lambda e: e.dma_start(out=scrAq, in_=wA[:, :, :, 0:448]), writes=["scrAq"], dma=True, own_sem=True)
        S.op(dma_w, lambda e: e.dma_start(out=scrAg, in_=wA[:, :, :, 448:704]), writes=["scrAg"], dma=True, own_sem=True)

    def late_casts_2():
        S.op(dma_w, lambda e: e.dma_start(out=wro[:], in_=wO3[0]), writes=["wro"], dma=True, own_sem=True)
        S.op(dma_w, lambda e: e.dma_start(out=scrM, in_=wM), writes=["scrM"], dma=True, own_sem=True)
        S.op(dma_w, lambda e: e.dma_start(out=wao[:], in_=wO3[1]), writes=["wao"], dma=True, own_sem=True)
        S.op(dma_w, lambda e: e.dma_start(out=wo[:], in_=wO3[2]), writes=["wo"], dma=True, own_sem=True)
        S.op("sp", lambda e: e.dma_start(out=gpost[:], in_=gbc[1]), writes=["gpost"], dma=True)
    S.op("dve", lambda e: e.memset(ones[:], 1.0), writes=["ones"])
    nhalf = sb("nhalf", [128, 8], F32)
    S.op("dve", lambda e: e.memset(nhalf[:], -0.5), writes=["nhalf"])
    negm = sb("negm", [128, 128], F32)
    S.op("dve", lambda e: e.memset(negm[:], -1.0e5), writes=["negm"])
    for g_ in range(4):
        S.op("dve", (lambda g_=g_: lambda e: e.memset(kbuf[g_][:, 0:128], 0.0))(), writes=[("kprev", g_)])
        S.op("dve", (lambda g_=g_: lambda e: e.memset(vbuf[g_][:, 0, :], 0.0))(), writes=[("vprev", g_)])
    S.op("dve", lambda e: e.memset(cstate[:], 0.0), writes=["cstate"])
    S.op("dve", lambda e: e.memset(hstate[:], 0.0), writes=["hstate"])
    S.op("dve", lambda e: e.tensor_scalar(out=dc[:, DC_HBA:DC_HBA + 16], in0=pv[:, PV_BA:PV_BA + 16],
                                          scalar1=0.5, scalar2=None, op0=ALU.mult),
         reads=["pv"], writes=["dc_hb"])
    S.op("dve", lambda e: e.tensor_scalar(out=dc[:, DC_HBG:DC_HBG + 16], in0=pv[:, PV_BG:PV_BG + 16],
                                          scalar1=0.5, scalar2=None, op0=ALU.mult),
         reads=["pv"], writes=["dc_hbg"])
    S.op("act", lambda e: e.activation(out=dc[:, DC_S:DC_S + 8], in_=pv[:, PV_LAM:PV_LAM + 8],
                                       func=AF.Exp, scale=-1.0),
         reads=["pv"], writes=["dc_s"])
    S.op("act", lambda e: e.activation(out=dc[:, DC_S:DC_S + 8], in_=dc[:, DC_S:DC_S + 8],
                                       func=AF.Ln, bias=1.0, scale=1.0),
         reads=["dc_s"], writes=["dc_s"])
    S.op("dve", lambda e: e.tensor_scalar(out=dc[:, DC_HS:DC_HS + 8], in0=dc[:, DC_S:DC_S + 8],
                                          scalar1=-4.0, scalar2=None, op0=ALU.mult),
         reads=["dc_s"], writes=["dc_hs"])
    S.op("dve", lambda e: e.tensor_scalar(out=dc[:, DC_S:DC_S + 8], in0=dc[:, DC_S:DC_S + 8],
                                          scalar1=-8.0, scalar2=None, op0=ALU.mult),
         reads=["dc_s", "dc_hs"], writes=["dc_s"])
    S.op("act", lambda e: e.activation(out=dc[:, DC_ESINK:DC_ESINK + 8], in_=pv[:, PV_SINK:PV_SINK + 8],
                                       func=AF.Exp),
         reads=["pv"], writes=["dc_esink"])
    S.op("dve", lambda e: e.tensor_scalar(out=dc[:, 56:64], in0=dc[:, DC_ESINK:DC_ESINK + 8],
                                          scalar1=2.0, scalar2=None, op0=ALU.mult),
         reads=["dc_esink"], writes=["dc_esink2"])

    piece_seq = []
    _units = [(kind, m) for m in range(8) for kind in "LRA"]
    _cost = {"L": 4.2, "R": 2.1, "A": 2.1}
    _rk = {k: (4.2 if k == -2 else 4.7 if k <= 5 else 0.5 if k == 6 else 0.0) for k in range(-2, 9)}
    _budget = (sum(_rk.values()) + sum(_cost[u[0]] for u in _units)) / 11.0
    M_SCHED = {}
    _ui = 0
    for _k in range(-2, 9):
        load = _rk[_k]
        M_SCHED[_k] = []
        while _ui < len(_units) and (_k == 8 or load + 0.5 * _cost[_units[_ui][0]] <= _budget):
            M_SCHED[_k].append(_units[_ui])
            load += _cost[_units[_ui][0]]
            _ui += 1

    def _rp(c):
        return (scrR[c], 256, "scrR0" if c < 2 else "scrR1")
    for _ch in range(nch):
        for _k in range(-2, 9):
            if 0 <= _k + 2 < 8:
                piece_seq.append(_rp(_k + 2))
            if _ch > 0:
                for _kind, _m in M_SCHED[_k]:
                    if _kind == "L":
                        piece_seq.append((scrM[_m], 256, "scrM"))
        for _g in range(4):
            piece_seq.append((scrAq[_g], 448, "scrAq"))
            piece_seq.append((scrAg[_g], 256, "scrAg"))
    piece_seq += [(scrM[m], 256, "scrM") for m in range(8)]
    piece_state = {"issued": 2, "used": 0}

    def issue_piece():
        n = piece_state["issued"]
        if n >= len(piece_seq):
            return
        piece_state["issued"] += 1
        src_ap, ncols, stok = piece_seq[n]
        i = n % 2
        S.op("sp", lambda e: e.dma_start(out=wv(i, ncols), in_=src_ap),
             reads=[stok], writes=[("wslot", i)], dma=True)

    def load_piece(src_ap, ncols):
        n = piece_state["used"]
        piece_state["used"] += 1
        while piece_state["issued"] <= n:
            issue_piece()
        if piece_state["issued"] <= n + 1:
            issue_piece()
        return n % 2

    big_loaded = [False]

    def load_big():
        if big_loaded[0]:
            return
        big_loaded[0] = True
        pass

    JUNK = 12

    def F_pre(ch, parts=(0, 1, 2)):
        T0 = ch * TC
        rp = (ch % 2) * 4

        def ld(tt):
            r0 = T0 + tt * 128
            S.op("sp", (lambda tt=tt, r0=r0: lambda e: e.dma_start(out=xt[tt % 2][:], in_=x[r0:r0 + 128, :]))(),
                 writes=[("xt", tt % 2)], dma=True)

        def proc(tt):
            sl = tt % 2
            col = rp + tt
            S.op("act", (lambda sl=sl, col=col: lambda e: e.activation(
                out=gbf(JUNK), in_=xt[sl][:], func=AF.Square, accum_out=ssf[:, col:col + 1]))(),
                 reads=[("xt", sl)], writes=[("G", JUNK), ("ssf", col)])
            S.op("pool", (lambda col=col: lambda e: e.tensor_scalar(
                out=rstf[:, col:col + 1], in0=ssf[:, col:col + 1], scalar1=1.0 / D, scalar2=EPS,
                op0=ALU.mult, op1=ALU.add))(),
                 reads=[("ssf", col)], writes=[("rstf", col)])
            S.op("pool", (lambda col=col: lambda e: e.tensor_tensor(
                out=rstf[:, col:col + 1], in0=rstf[:, col:col + 1], in1=nhalf[:, 0:1], op=ALU.pow))(),
                 reads=[("rstf", col), "nhalf"], writes=[("rstf", col)])
            S.op("dve", (lambda tt=tt, sl=sl, col=col: lambda e: e.scalar_tensor_tensor(
                out=hbv(tt), in0=xt[sl][:], scalar=rstf[:, col:col + 1], in1=gpre[:],
                op0=ALU.mult, op1=ALU.mult))(),
                 reads=[("xt", sl), ("rstf", col), "gpre"], writes=[("G", HB0 + tt)])
        if 0 in parts:
            ld(0)
            ld(1)
        if 1 in parts:
            proc(0)
            proc(1)
            ld(2)
            ld(3)
        if 2 in parts:
            proc(2)
            proc(3)

    def F_tr(ch):
        for tt in range(4):
            pb = tt % 2

            def tr(e, tt=tt, pb=pb):
                pbv = bank(pb).bitcast(BF16)
                ins = None
                for d in range(8):
                    ins = e.transpose(pbv[:, d * 128:(d + 1) * 128], hbv(tt)[:, d * 128:(d + 1) * 128], ident[:])
                return ins
            S.op("pe", tr, reads=[("G", HB0 + tt), "ident"], writes=[btok(pb)])
            S.op("act", (lambda tt=tt, pb=pb, hTc=hTs[ch % 2]: lambda e: e.activation(
                out=hTc[:, :, tt * 128:(tt + 1) * 128],
                in_=bank(pb).bitcast(BF16).rearrange("p (d t) -> p d t", d=8),
                func=AF.Identity))(),
                 reads=[btok(pb)], writes=[("hT", ch % 2)])

    def R_tiles(c):
        par = c % 2
        gb = par * 13
        return par, gb, par * 4

    def RB_G(c):
        return 2 + (c % 2)

    def RB_X(c):
        return c % 2
    RB_R, RB_I = 4, 5
    MB = [6, 7]

    def R0(c):
        ws = load_piece(wR[c], 256)

        p = CH["p"]

        def mm(col0, b, ws=ws, hTc=hTs[p]):
            def f(e):
                ins = None
                for d in range(8):
                    ins = e.matmul(bank(b), lhsT=wv(ws, 256)[:, d, col0:col0 + 128], rhs=hTc[:, d, :],
                                   start=(d == 0), stop=(d == 7))
                return ins
            return f
        S.op("pe", mm(0, RB_X(c)), reads=[("wslot", ws), ("hT", p)], writes=[btok(RB_X(c))])
        S.op("pe", mm(128, RB_G(c)), reads=[("wslot", ws), ("hT", p)], writes=[btok(RB_G(c))])

    def R1(c):
        par, gb, _pb = R_tiles(c)
        pbase = RB_X(c)
        xr_ = xr[par]
        cv = gb + 0

        S.op(EW, (lambda c=c, xr_=xr_: lambda e: e.tensor_copy(out=xr_[:, 0:3], in_=cstate[:, c, 0:3]))(),
             reads=["cstate"], writes=[("xrh", par)])
        S.op("act", (lambda xr_=xr_, pbase=pbase: lambda e: e.activation(
            out=xr_[:, 3:3 + TC], in_=bank(pbase), func=AF.Identity))(),
             reads=[btok(pbase)], writes=[("xr", par)])
        S.op(EW, (lambda c=c, xr_=xr_: lambda e: e.tensor_copy(out=cstate[:, c, 0:3], in_=xr_[:, TC:TC + 3]))(),
             reads=[("xr", par)], writes=["cstate"])
        S.op("dve", (lambda c=c, xr_=xr_, cv=cv: lambda e: e.tensor_scalar(
            out=g32(cv), in0=xr_[:, 3:3 + TC], scalar1=pv[:, PV_CW + c * 4 + 3:PV_CW + c * 4 + 4],
            scalar2=pv[:, PV_CB + c:PV_CB + c + 1], op0=ALU.mult, op1=ALU.add))(),
             reads=[("xr", par), "pv"], writes=[("G", cv)])
    def R1b(c):
        par, gb, pbase = R_tiles(c)
        xr_ = xr[par]
        cv = gb + 0
        for k in (2, 1, 0):
            S.op("dve", (lambda c=c, xr_=xr_, cv=cv, k=k: lambda e: e.scalar_tensor_tensor(
                out=g32(cv), in0=xr_[:, k:k + TC], scalar=pv[:, PV_CW + c * 4 + k:PV_CW + c * 4 + k + 1],
                in1=g32(cv), op0=ALU.mult, op1=ALU.add))(),
                 reads=[("xr", par), ("xrh", par), "pv", ("G", cv)], writes=[("G", cv)])

    def R2(c):
        par, gb, _pb = R_tiles(c)
        pb1, pb2, pb3 = RB_G(c), RB_R, RB_I
        cv, tr_, ti_, a_, e2_, hm_, w_, u_, y_, tg_, tmp2_, cvb_i = [gb + k for k in range(12)]
        S.op("act", (lambda cv=cv, cvb_i=cvb_i: lambda e: e.activation(
            out=gbf(cvb_i)[:, 0:TC], in_=g32(cv), func=AF.Identity))(),
             reads=[("G", cv)], writes=[("G", cvb_i)])
        S.op("pe", (lambda c=c, cvb_i=cvb_i, pb1=pb1, pb2=pb2, pb3=pb3: lambda e: e.matmul(
            bank(pb2), lhsT=wga[:, c, :], rhs=gbf(cvb_i)[:, 0:TC], start=True, stop=True))(),
             reads=["wga", ("G", cvb_i)], writes=[btok(pb2)])
        S.op("pe", (lambda c=c, cvb_i=cvb_i, pb1=pb1, pb2=pb2, pb3=pb3: lambda e: e.matmul(
            bank(pb3), lhsT=wgx[:, c, :], rhs=gbf(cvb_i)[:, 0:TC], start=True, stop=True))(),
             reads=["wgx", ("G", cvb_i)], writes=[btok(pb3)])
        S.op("act", (lambda c=c, tr_=tr_, pb1=pb1, pb2=pb2, pb3=pb3: lambda e: e.activation(
            out=g32(tr_), in_=bank(pb2), func=AF.Tanh, scale=0.5,
            bias=dc[:, DC_HBA + c:DC_HBA + c + 1]))(),
             reads=[btok(pb2), "dc_hb"], writes=[("G", tr_)])
        S.op("act", (lambda c=c, ti_=ti_, pb1=pb1, pb2=pb2, pb3=pb3: lambda e: e.activation(
            out=g32(ti_), in_=bank(pb3), func=AF.Tanh, scale=0.5,
            bias=dc[:, DC_HBX + c:DC_HBX + c + 1]))(),
             reads=[btok(pb3), "dc_hb"], writes=[("G", ti_)])
        S.op("act", (lambda c=c, tr_=tr_, a_=a_: lambda e: e.activation(
            out=g32(a_), in_=g32(tr_), func=AF.Exp, scale=dc[:, DC_HS + c:DC_HS + c + 1],
            bias=dc[:, DC_HS + c:DC_HS + c + 1]))(),
             reads=[("G", tr_), "dc_hs"], writes=[("G", a_)])
        S.op("act", (lambda c=c, tr_=tr_, e2_=e2_: lambda e: e.activation(
            out=g32(e2_), in_=g32(tr_), func=AF.Exp, scale=dc[:, DC_S + c:DC_S + c + 1],
            bias=dc[:, DC_S + c:DC_S + c + 1]))(),
             reads=[("G", tr_), "dc_s"], writes=[("G", e2_)])
        S.op("act", (lambda tg_=tg_, pb1=pb1, pb2=pb2, pb3=pb3: lambda e: e.activation(
            out=g32(tg_), in_=bank(pb1), func=AF.Tanh, scale=0.5))(),
             reads=[btok(pb1)], writes=[("G", tg_)])
        S.op("dve", (lambda ti_=ti_, cv=cv, w_=w_: lambda e: e.scalar_tensor_tensor(
            out=g32(w_), in0=g32(ti_), scalar=1.0, in1=g32(cv), op0=ALU.add, op1=ALU.mult))(),
             reads=[("G", ti_), ("G", cv)], writes=[("G", w_)])
        S.op("dve", (lambda tg_=tg_, tmp2_=tmp2_, pb1=pb1, pb2=pb2, pb3=pb3: lambda e: e.scalar_tensor_tensor(
            out=g32(tmp2_), in0=g32(tg_), scalar=1.0, in1=bank(pb1), op0=ALU.add, op1=ALU.mult))(),
             reads=[("G", tg_), btok(pb1)], writes=[("G", tmp2_)])

    def R3a(c):
        par, gb, pbase = R_tiles(c)
        e2_, hm_ = gb + 4, gb + 5
        S.op("act", (lambda e2_=e2_, hm_=hm_: lambda e: e.activation(
            out=g32(hm_), in_=g32(e2_), func=AF.Sqrt, scale=-0.25, bias=0.25))(),
             reads=[("G", e2_)], writes=[("G", hm_)])

    def R3b(c):
        par, gb, pbase = R_tiles(c)
        cv, tr_, ti_, a_, e2_, hm_, w_, u_, y_, tg_, tmp2_, cvb_i = [gb + k for k in range(12)]
        S.op("dve", (lambda hm_=hm_, w_=w_, u_=u_: lambda e: e.tensor_tensor(
            out=g32(u_), in0=g32(hm_), in1=g32(w_), op=ALU.mult))(),
             reads=[("G", hm_), ("G", w_)], writes=[("G", u_)])
        S.op("dve", (lambda c=c, a_=a_, u_=u_, y_=y_: lambda e: e.tensor_tensor_scan(
            out=g32(y_), data0=g32(a_), data1=g32(u_), initial=hstate[:, c:c + 1],
            op0=ALU.mult, op1=ALU.add))(),
             reads=[("G", a_), ("G", u_), "hstate"], writes=[("G", y_)])
        S.op(EW, (lambda c=c, y_=y_: lambda e: e.tensor_copy(out=hstate[:, c:c + 1], in_=g32(y_)[:, TC - 1:TC]))(),
             reads=[("G", y_)], writes=["hstate"])
        S.op("dve", (lambda c=c, tmp2_=tmp2_, y_=y_, yg=ygrs[CH["p"]]: lambda e: e.scalar_tensor_tensor(
            out=yg[:, c, :], in0=g32(tmp2_), scalar=0.5, in1=g32(y_), op0=ALU.mult, op1=ALU.mult))(),
             reads=[("G", tmp2_), ("G", y_)], writes=[("ygr", CH["p"], c)])

    A_SC = [0, 1]
    A_SG = [[2, 3], [4, 5]]
    A_TGA, A_DS, A_YA = 6, 17, 25
    PT0 = 7
    a_rot = [0]
    a_ws = {}

    def A_P(g):
        p = CH["p"]
        hTc = hTs[p]
        ws = load_piece(None, 448)
        kb_, vb_ = kbuf[g], vbuf[g]

        def nextbank():
            b = a_rot[0] % 2
            a_rot[0] += 1
            return b

        def proj(col0, b, ws, nc_=448):
            def f(e, ws=ws, hTc=hTc, nc_=nc_):
                ins = None
                for d in range(8):
                    ins = e.matmul(bank(b), lhsT=wv(ws, nc_)[:, d, col0:col0 + 128], rhs=hTc[:, d, :],
                                   start=(d == 0), stop=(d == 7))
                return ins
            return f
        for j in range(2):
            b = nextbank()
            S.op("pe", proj(j * 128, b, ws), reads=[("wslot", ws), ("hT", p)], writes=[btok(b)])
            S.op("act", (lambda j=j, b=b: lambda e: e.activation(
                out=qT[:, j, :], in_=bank(b), func=AF.Identity, scale=0.125))(),
                 reads=[btok(b)], writes=[("qT", j)])
        b = nextbank()
        S.op("pe", proj(256, b, ws), reads=[("wslot", ws), ("hT", p)], writes=[btok(b)])
        S.op("act", (lambda kb_=kb_, b=b: lambda e: e.activation(
            out=kb_[:, 128:128 + TC], in_=bank(b), func=AF.Identity))(),
             reads=[btok(b)], writes=[("kcur", g)])
        b = nextbank()

        def mm_v(e, ws=ws, b=b, hTc=hTc):
            ins = None
            for tt in range(4):
                for d in range(8):
                    ins = e.matmul(bank(b)[:, tt * 64:(tt + 1) * 64], lhsT=hTc[:, d, tt * 128:(tt + 1) * 128],
                                   rhs=wv(ws, 448)[:, d, 384:448], start=(d == 0), stop=(d == 7))
            return ins
        S.op("pe", mm_v, reads=[("wslot", ws), ("hT", p)], writes=[btok(b)])
        S.op("act", (lambda vb_=vb_, b=b: lambda e: e.activation(
            out=vb_[:, 1:5, :], in_=bank(b)[:, 0:256].rearrange("p (t d) -> p t d", t=4),
            func=AF.Identity))(),
             reads=[btok(b)], writes=[("vcur", g)])
        ws2 = load_piece(None, 256)
        for j in range(2):
            b = nextbank()
            sg = A_SG[g % 2][j]
            S.op("pe", proj(j * 128, b, ws2, 256), reads=[("wslot", ws2), ("hT", p)], writes=[btok(b)])
            S.op("act", (lambda b=b: lambda e: e.activation(
                out=g32(A_TGA), in_=bank(b), func=AF.Tanh, scale=0.5))(),
                 reads=[btok(b)], writes=[("G", A_TGA)])
            S.op("dve", (lambda sg=sg, b=b: lambda e: e.scalar_tensor_tensor(
                out=g32(sg), in0=g32(A_TGA), scalar=1.0, in1=bank(b), op0=ALU.add, op1=ALU.mult))(),
                 reads=[("G", A_TGA), btok(b)], writes=[("G", sg)])

    def A_S(g, first):
        kb_ = kbuf[g]
        srot = 0
        for kb in range(5):
            if kb == 0:
                q0, w_, msk = 0, 128, (negm[:, 0:128] if first else dmask[:, 128:256])
            elif kb == 4:
                q0, w_, msk = 384, 128, dmask[:, 0:128]
            else:
                q0, w_, msk = (kb - 1) * 128, 256, dmask[:, 0:256]
            for ph in range(2):
                sbk = 2 + (srot % 2)
                sci = A_SC[srot % 2]
                srot += 1
                pr = slice(ph * 64, (ph + 1) * 64)
                S.op("pe", (lambda kb=kb, pr=pr, q0=q0, w_=w_, sbk=sbk, kb_=kb_: lambda e: e.matmul(
                    bank(sbk)[:, 0:2 * w_], lhsT=kb_[pr, kb * 128:(kb + 1) * 128], rhs=qT[pr, :, q0:q0 + w_],
                    start=True, stop=True))(),
                     reads=[("kcur", g), ("kprev", g), ("qT", 0), ("qT", 1)], writes=[btok(sbk)])
                for j in range(2):
                    head = 4 * g + 2 * j + ph
                    S.op("dve", (lambda j=j, head=head, w_=w_, sbk=sbk, sci=sci, msk=msk: lambda e: e.scalar_tensor_tensor(
                        out=g32(sci)[:, j * w_:(j + 1) * w_], in0=msk, scalar=float(SLOPES[head]),
                        in1=bank(sbk)[:, j * w_:(j + 1) * w_], op0=ALU.mult, op1=ALU.add))(),
                         reads=["dmask", "negm", btok(sbk)], writes=[("G", sci)])
                S.op("act", (lambda kb=kb, ph=ph, w_=w_, sci=sci: lambda e: e.activation(
                    out=gbf(PT0 + kb)[:, ph * 512:ph * 512 + 2 * w_], in_=g32(sci)[:, 0:2 * w_], func=AF.Exp))(),
                     reads=[("G", sci)], writes=[("PT", kb, ph), ("G", PT0 + kb)])

    def A_V(g):
        vb_ = vbuf[g]
        kb_ = kbuf[g]
        Ot, Dt = PS2[2], PS2[3]

        def ptview(kb, ph, part):
            base = gbf(PT0 + kb)[:, ph * 512:(ph + 1) * 512]
            if kb == 0 or kb == 4:
                return base[:, 0:256]
            return base.rearrange("p (j q) -> p j q", j=2)[:, :, part * 128:(part + 1) * 128]

        def mm_pv(e):
            ins = None
            for i in range(4):
                for ph in range(2):
                    pr = slice(ph * 64, (ph + 1) * 64)
                    terms = [(i, 1), (i + 1, 0)]
                    for ti, (kb, part) in enumerate(terms):
                        e.matmul(Ot[pr, i * 256:(i + 1) * 256], lhsT=vb_[:, kb, :], rhs=ptview(kb, ph, part),
                                 start=(ti == 0), stop=(ti == 1))
                    for ti, (kb, part) in enumerate(terms):
                        ins = e.matmul(Dt[pr, i * 256:(i + 1) * 256], lhsT=ones[:, 0:64], rhs=ptview(kb, ph, part),
                                       start=(ti == 0), stop=(ti == 1))
            return ins
        S.op("pe", mm_pv,
             reads=[("vcur", g), ("vprev", g), "ones"] + [("PT", kb, ph) for kb in range(5) for ph in range(2)]
             + [("G", PT0 + kb) for kb in range(5)],
             writes=[btok(4), btok(5), btok(6), btok(7)])
    def A_N(g):
        vb_ = vbuf[g]
        kb_ = kbuf[g]
        Ot, Dt = PS2[2], PS2[3]
        Ov = Ot[:].rearrange("p (i j q) -> p i j q", i=4, j=2)
        Dv = Dt[:].rearrange("p (i j q) -> p i j q", i=4, j=2)
        DSs = [17, 18]
        YAs = [25, 19]
        for j in range(2):
            cc = 2 * g + j
            S.op("act", (lambda j=j, cc=cc: lambda e: e.activation(
                out=g32(DSs[j]).rearrange("p (i q) -> p i q", i=4), in_=Dv[:, :, j, :], func=AF.Ln,
                scale=2.0, bias=dc[:, 56 + cc:56 + cc + 1]))(),
                 reads=[btok(6), btok(7), "dc_esink2"], writes=[("G", DSs[j])])
        for j in range(2):
            S.op("act", (lambda j=j: lambda e: e.activation(
                out=g32(DSs[j]), in_=g32(DSs[j]), func=AF.Exp, scale=-1.0))(),
                 reads=[("G", DSs[j])], writes=[("G", DSs[j])])
        for j in range(2):
            S.op("dve", (lambda j=j: lambda e: e.tensor_tensor(
                out=g32(YAs[j]).rearrange("p (i q) -> p i q", i=4), in0=Ov[:, :, j, :],
                in1=g32(DSs[j]).rearrange("p (i q) -> p i q", i=4), op=ALU.mult))(),
                 reads=[btok(4), btok(5), ("G", DSs[j])], writes=[("G", YAs[j])])
        for j in range(2):
            cc = 2 * g + j
            sg = A_SG[g % 2][j]
            S.op("dve", (lambda j=j, cc=cc, sg=sg: lambda e: e.tensor_tensor(
                out=yga[:, cc, :], in0=g32(YAs[j]), in1=g32(sg), op=ALU.mult))(),
                 reads=[("G", YAs[j]), ("G", sg)], writes=[("yga", cc)])
        S.op(EW, (lambda kb_=kb_: lambda e: e.tensor_copy(out=kb_[:, 0:128], in_=kb_[:, TC:TC + 128]))(),
             reads=[("kcur", g)], writes=[("kprev", g)])
        S.op(EW, (lambda vb_=vb_: lambda e: e.tensor_copy(out=vb_[:, 0, :], in_=vb_[:, 4, :]))(),
             reads=[("vcur", g)], writes=[("vprev", g)])

    m_ws = {}

    def M_a(m, p, part="LR"):
        par = m % 2
        ws = load_piece(None, 256) if "L" in part else None
        hTc, ygc = hTs[p], ygrs[p]
        t_r, t_a = Mt[:, par * 2 + 0, :], Mt[:, par * 2 + 1, :]
        tk_r, tk_a = ("Mt", par, 0), ("Mt", par, 1)

        def mm_l(col0, b, ws=ws, hTc=hTc):
            def f(e):
                ins = None
                for d in range(8):
                    ins = e.matmul(bank(b), lhsT=wv(ws, 256)[:, d, col0:col0 + 128], rhs=hTc[:, d, :],
                                   start=(d == 0), stop=(d == 7))
                return ins
            return f

        def mm_b(wt, src, b, m=m):
            def f(e):
                ins = None
                for c in range(8):
                    ins = e.matmul(bank(b), lhsT=wt[:, c, m * 128:(m + 1) * 128], rhs=src[:, c, :],
                                   start=(c == 0), stop=(c == 7))
                return ins
            return f
        if "L" in part:
            S.op("pe", mm_l(0, MB[0]), reads=[("wslot", ws), ("hT", p)], writes=[btok(MB[0])])
            S.op("pe", mm_l(128, MB[1]), reads=[("wslot", ws), ("hT", p)], writes=[btok(MB[1])])
            S.op("act", (lambda m=m, t_r=t_r: lambda e: e.activation(
                out=t_r, in_=bank(MB[0]), func=AF.Tanh, scale=0.5, bias=dc[:, DC_HBG + m:DC_HBG + m + 1]))(),
                 reads=[btok(MB[0]), "dc_hbg"], writes=[tk_r])
            S.op("act", (lambda m=m, t_a=t_a: lambda e: e.activation(
                out=t_a, in_=bank(MB[1]), func=AF.Tanh, scale=0.5,
                bias=dc[:, DC_HBG + 8 + m:DC_HBG + 8 + m + 1]))(),
                 reads=[btok(MB[1]), "dc_hbg"], writes=[tk_a])
        if "R" not in part:
            return
        S.op("pe", mm_b(wro, ygc, MB[0]), reads=["wro"] + [("ygr", p, c) for c in range(8)],
             writes=[btok(MB[0])])
        S.op("dve", (lambda t_r=t_r: lambda e: e.scalar_tensor_tensor(
            out=t_r, in0=t_r, scalar=1.0, in1=bank(MB[0]), op0=ALU.add, op1=ALU.mult))(),
             reads=[tk_r, btok(MB[0])], writes=[tk_r])

    def M_b(m, p):
        par = m % 2
        t_r, t_a = Mt[:, par * 2 + 0, :], Mt[:, par * 2 + 1, :]
        tk_r, tk_a = ("Mt", par, 0), ("Mt", par, 1)

        def mm_b(wt, src, b, m=m):
            def f(e):
                ins = None
                for c in range(8):
                    ins = e.matmul(bank(b), lhsT=wt[:, c, m * 128:(m + 1) * 128], rhs=src[:, c, :],
                                   start=(c == 0), stop=(c == 7))
                return ins
            return f
        S.op("pe", mm_b(wao, yga, MB[1]), reads=["wao"] + [("yga", c) for c in range(8)],
             writes=[btok(MB[1])])
        S.op("dve", (lambda t_a=t_a: lambda e: e.scalar_tensor_tensor(
            out=t_a, in0=t_a, scalar=1.0, in1=bank(MB[1]), op0=ALU.add, op1=ALU.mult))(),
             reads=[tk_a, btok(MB[1])], writes=[tk_a])
        S.op("dve", (lambda m=m, t_r=t_r, t_a=t_a: lambda e: e.tensor_tensor(
            out=mrg[:, m, :], in0=t_r, in1=t_a, op=ALU.add))(),
             reads=[tk_r, tk_a], writes=[("mrg", m)])

    OT0 = 14

    def O_a(ch, tt):
        pbase = 4 + (tt % 2) * 2
        for hf in range(2):
            def mm_o(e, tt=tt, hf=hf, pbase=pbase):
                ins = None
                for m in range(8):
                    ins = e.matmul(bank(pbase + hf), lhsT=mrg[:, m, tt * 128:(tt + 1) * 128],
                                   rhs=wo[:, m, hf * 512:(hf + 1) * 512], start=(m == 0), stop=(m == 7))
                return ins
            S.op("pe", mm_o, reads=["wo"] + [("mrg", m) for m in range(8)], writes=[btok(pbase + hf)])
            S.op("act", (lambda tt=tt, hf=hf, pbase=pbase: lambda e: e.activation(
                out=gbf(JUNK)[:, 0:512], in_=bank(pbase + hf), func=AF.Square,
                accum_out=sso[:, tt * 2 + hf:tt * 2 + hf + 1]))(),
                 reads=[btok(pbase + hf)], writes=[("G", JUNK), ("sso", tt, hf)])
        S.op("dve", (lambda tt=tt: lambda e: e.tensor_tensor(
            out=sso[:, 8 + tt:9 + tt], in0=sso[:, tt * 2:tt * 2 + 1], in1=sso[:, tt * 2 + 1:tt * 2 + 2],
            op=ALU.add))(),
             reads=[("sso", tt, 0), ("sso", tt, 1)], writes=[("ssr", tt)])
        S.op("pool", (lambda tt=tt: lambda e: e.tensor_scalar(
            out=sso[:, 8 + tt:9 + tt], in0=sso[:, 8 + tt:9 + tt], scalar1=1.0 / D, scalar2=4.0 * EPS,
            op0=ALU.mult, op1=ALU.add))(),
             reads=[("ssr", tt)], writes=[("ssr", tt)])
        S.op("pool", (lambda tt=tt: lambda e: e.tensor_tensor(
            out=sso[:, 8 + tt:9 + tt], in0=sso[:, 8 + tt:9 + tt], in1=nhalf[:, 0:1], op=ALU.pow))(),
             reads=[("ssr", tt), "nhalf"], writes=[("ssr", tt)])

    def O_b1(ch, tt):
        pbase = 4 + (tt % 2) * 2
        ot = G[:, OT0 + 2 * tt:OT0 + 2 * tt + 2, :]
        ottok = [("G", OT0 + 2 * tt), ("G", OT0 + 2 * tt + 1)]
        for hf in range(2):
            S.op("dve", (lambda tt=tt, hf=hf, pbase=pbase, ot=ot: lambda e: e.scalar_tensor_tensor(
                out=ot[:, hf, :], in0=bank(pbase + hf), scalar=sso[:, 8 + tt:9 + tt],
                in1=gpost[:, hf * 512:(hf + 1) * 512], op0=ALU.mult, op1=ALU.mult))(),
                 reads=[btok(pbase + hf), ("ssr", tt), "gpost"], writes=[ottok[hf]])

    def O_ld(ch, tt):
        r0 = ch * TC + tt * 128
        par = tt % 2
        xr_ap = G[:, 22 + 2 * par:24 + 2 * par, :].rearrange("p a b -> p (a b)")
        S.op("sp", (lambda r0=r0, xr_ap=xr_ap: lambda e: e.dma_start(out=xr_ap, in_=x[r0:r0 + 128, :]))(),
             writes=[("G", 22 + 2 * par), ("G", 23 + 2 * par)], dma=True)

    def O_b2(ch, tt):
        r0 = ch * TC + tt * 128
        par = tt % 2
        otf = G[:, OT0 + 2 * tt:OT0 + 2 * tt + 2, :].rearrange("p a b -> p (a b)")
        ottok = [("G", OT0 + 2 * tt), ("G", OT0 + 2 * tt + 1)]
        xr_ap = G[:, 22 + 2 * par:24 + 2 * par, :].rearrange("p a b -> p (a b)")
        S.op("dve", (lambda otf=otf, xr_ap=xr_ap: lambda e: e.tensor_tensor(out=otf, in0=otf, in1=xr_ap, op=ALU.add))(),
             reads=ottok + [("G", 22 + 2 * par), ("G", 23 + 2 * par)], writes=ottok)
        S.op("sp", (lambda r0=r0, otf=otf: lambda e: e.dma_start(out=out[r0:r0 + 128, :], in_=otf))(),
             reads=ottok, dma=True)

    def ph(name, f, *a):
        S.stage = name
        f(*a)

    def R_iter(k):
        if 0 <= k + 2 < 8:
            ph("R0(%d)" % (k + 2), R0, k + 2)
            ph("R1(%d)" % (k + 2), R1, k + 2)
            ph("R1b(%d)" % (k + 2), R1b, k + 2)
        if k % 2 == 0:
            pair = [cc for cc in (k - 1, k) if 0 <= cc < 8]
            for cc in pair:
                ph("R3a(%d)" % cc, R3a, cc)
            for cc in pair:
                ph("R3b(%d)" % cc, R3b, cc)
        if 0 <= k + 1 < 8:
            ph("R2(%d)" % (k + 1), R2, k + 1)

    def O_stage(ch):
        ph("Oa(0)", O_a, ch, 0)
        ph("Oa(1)", O_a, ch, 1)
        ph("Old", O_ld, ch, 0)
        ph("Old", O_ld, ch, 1)
        ph("Ob1(0)", O_b1, ch, 0)
        ph("Ob1(1)", O_b1, ch, 1)
        ph("Oa(2)", O_a, ch, 2)
        ph("Oa(3)", O_a, ch, 3)
        ph("Ob2(0)", O_b2, ch, 0)
        ph("Ob2(1)", O_b2, ch, 1)
        ph("Old", O_ld, ch, 2)
        ph("Old", O_ld, ch, 3)
        ph("Ob1(2)", O_b1, ch, 2)
        ph("Ob1(3)", O_b1, ch, 3)
        ph("Ob2(2)", O_b2, ch, 2)
        ph("Ob2(3)", O_b2, ch, 3)

    ph("Fpre", F_pre, 0)
    late_casts_1()
    late_casts_2()
    ph("Ftr", F_tr, 0)
    for ch in range(nch):
        first = ch == 0
        CH["p"] = ch % 2
        for k in range(-2, 9):
            R_iter(k)
            if ch > 0:
                for kind, m in M_SCHED[k]:
                    if kind == "A":
                        ph("Mb(%d)" % m, M_b, m, (ch - 1) % 2)
                    else:
                        ph("M%s(%d)" % (kind, m), M_a, m, (ch - 1) % 2, kind)
        po = ch - 1
        if ch > 0:
            ph("Oa(0)", O_a, po, 0)
            ph("Oa(1)", O_a, po, 1)
            ph("Old", O_ld, po, 0)
            ph("Old", O_ld, po, 1)
        ph("AP(0)", A_P, 0)
        if ch > 0:
            ph("Ob1(0)", O_b1, po, 0)
            ph("Ob1(1)", O_b1, po, 1)
            ph("Oa(2)", O_a, po, 2)
            ph("Oa(3)", O_a, po, 3)
        ph("AS(0)", A_S, 0, first)
        if ch > 0:
            ph("Ob2(0)", O_b2, po, 0)
            ph("Ob2(1)", O_b2, po, 1)
            ph("Old", O_ld, po, 2)
            ph("Old", O_ld, po, 3)
            ph("Ob1(2)", O_b1, po, 2)
            ph("Ob1(3)", O_b1, po, 3)
        for g in range(4):
            if g + 1 < 4:
                ph("AP(%d)" % (g + 1), A_P, g + 1)
            if g == 0 and ch > 0:
                ph("Ob2(2)", O_b2, po, 2)
                ph("Ob2(3)", O_b2, po, 3)
            ph("AV(%d)" % g, A_V, g)
            if g + 1 < 4:
                ph("AS(%d)" % (g + 1), A_S, g + 1, first)
            ph("AN(%d)" % g, A_N, g)
            if g < 3 and ch + 1 < nch:
                ph("Fpre%d" % g, F_pre, ch + 1, (g,))
        if ch + 1 < nch:
            ph("Ftr", F_tr, ch + 1)
    for m in range(8):
        ph("Ma(%d)" % m, M_a, m, (nch - 1) % 2)
        ph("Mb(%d)" % m, M_b, m, (nch - 1) % 2)
    O_stage(nch - 1)

    stores = [o.idx for o in S.ops if o.dma]
    fin = S.op("sp", lambda e: e.nop())
    fin.deps.update(stores)
    S.emit()
    return nc


def _layouts(pre_norm_g, w_in, b_gate, conv_w, conv_b, w_rg_a, b_rg_a, w_rg_x, b_rg_x,
             lru_lambda, attn_sinks, w_rnn_out, w_attn_out, w_out, post_norm_g):
    f = np.float32
    W = np.ascontiguousarray(w_in[0], dtype=f)
    Wd = W.reshape(8, 128, 6656)
    o_rx, o_rg, o_q, o_k, o_v, o_ag, o_ml = 0, 1024, 2048, 3072, 3328, 3584, 4608

    def cols(c0, n):
        return Wd[:, :, c0:c0 + n].transpose(1, 0, 2)

    wR = np.stack([np.concatenate([cols(o_rx + c * 128, 128), cols(o_rg + c * 128, 128)], axis=2)
                   for c in range(8)]).astype(f)
    wA = np.stack([np.concatenate([cols(o_q + g * 256, 256), cols(o_k + g * 64, 64), cols(o_k + g * 64, 64),
                                   cols(o_v + g * 64, 64), cols(o_ag + g * 256, 256)], axis=2)
                   for g in range(4)]).astype(f)
    wM = np.stack([np.concatenate([cols(o_ml + m * 128, 128), cols(o_ml + 1024 + m * 128, 128)], axis=2)
                   for m in range(8)]).astype(f)

    def rows(w):
        return np.ascontiguousarray(w[0].reshape(8, 128, 1024).transpose(1, 0, 2), dtype=f)
    wO3 = np.stack([rows(w_rnn_out), rows(w_attn_out), rows(w_out)])
    wG = np.zeros((2, 128, 8, 128), f)
    for i, wg in enumerate((w_rg_a[0], w_rg_x[0])):
        for c in range(8):
            wG[i, 0:64, c, 0:64] = wg[2 * c]
            wG[i, 64:128, c, 64:128] = wg[2 * c + 1]
    pvec = np.zeros((128, PV_N), f)
    pvec[:, PV_CW:PV_CW + 32] = conv_w[0].reshape(4, 8, 128).transpose(2, 1, 0).reshape(128, 32)
    pvec[:, PV_CB:PV_CB + 8] = conv_b[0].reshape(8, 128).T
    pvec[:, PV_BA:PV_BA + 8] = b_rg_a[0].reshape(8, 128).T
    pvec[:, PV_BX:PV_BX + 8] = b_rg_x[0].reshape(8, 128).T
    pvec[:, PV_LAM:PV_LAM + 8] = lru_lambda[0].reshape(8, 128).T
    sk = attn_sinks[0].reshape(8, 2)
    pvec[0:64, PV_SINK:PV_SINK + 8] = sk[:, 0][None, :]
    pvec[64:128, PV_SINK:PV_SINK + 8] = sk[:, 1][None, :]
    pvec[:, PV_BG:PV_BG + 16] = b_gate[0].reshape(16, 128).T
    gbc = np.stack([np.broadcast_to(pre_norm_g[0][None, :], (128, 1024)),
                    np.broadcast_to(post_norm_g[0][None, :], (128, 1024))]).astype(f)
    kk = np.arange(128)[:, None]
    qq = np.arange(128)[None, :]
    dm = np.empty((128, 256), f)
    dm[:, 0:128] = np.where(qq >= kk, -(qq - kk), -1.0e5)
    dm[:, 128:256] = np.where(qq < kk, -(128 + qq - kk), -1.0e5)
    return dict(wR=wR, wA=wA, wM=wM, wO3=wO3, wG=wG, pvec=pvec, gbc=np.ascontiguousarray(gbc), dmask=dm)


_NC_CACHE = {}


def kernel(x, pre_norm_g, w_in, b_gate, conv_w, conv_b, w_rg_a, b_rg_a, w_rg_x, b_rg_x,
           lru_lambda, attn_sinks, w_rnn_out, w_attn_out, w_out, post_norm_g):
    x = np.asarray(x, dtype=np.float32)
    shared = _layouts(*[np.asarray(a, dtype=np.float32) for a in (
        pre_norm_g, w_in, b_gate, conv_w, conv_b, w_rg_a, b_rg_a, w_rg_x, b_rg_x,
        lru_lambda, attn_sinks, w_rnn_out, w_attn_out, w_out, post_norm_g)])
    nc = build_program()
    in_maps = []
    for b in range(N_CORES):
        m = dict(shared)
        m["x"] = np.ascontiguousarray(x[b])
        in_maps.append(m)
    res = run_bass_kernel_spmd(nc, in_maps, core_ids=list(range(N_CORES)))
    return np.stack([np.asarray(r["out"], dtype=np.float32) for r in res.results], axis=0)
```

```python
import numpy as np
from contextlib import ExitStack
import concourse.bass as bass
import concourse.mybir as mybir
from concourse.bass_utils import run_bass_kernel_spmd

F32 = mybir.dt.float32
BF16 = mybir.dt.bfloat16
AF = mybir.ActivationFunctionType
ALU = mybir.AluOpType

D = 1024
T = 4096
TC = 512
NCH = T // TC
EPS = 1e-6
N_CORES = 8
ENGINES = ("pe", "act", "dve", "pool", "sp")


class Op:
    __slots__ = ("idx", "eng", "fn", "deps", "dma", "signal", "semid", "semval", "tag")

    def __init__(self, idx, eng, fn, dma):
        self.idx = idx
        self.eng = eng
        self.fn = fn
        self.dma = dma
        self.deps = set()
        self.signal = False
        self.semid = None
        self.semval = 0


class Sched:
    N_DMA_SEMS = 24
    count_pe = False

    def __init__(self, nc):
        self.nc = nc
        self.ops = []
        self.last_writer = {}
        self.readers = {}
        self.dma_count = 0
        self.dma_last_on_sem = {}
        self.pe_map = []

    stage = ""

    def op(self, eng, fn, reads=(), writes=(), dma=False, own_sem=False):
        o = Op(len(self.ops), eng, fn, dma)
        o.tag = self.stage
        for t in reads:
            w = self.last_writer.get(t)
            if w is not None:
                o.deps.add(w)
        for t in writes:
            w = self.last_writer.get(t)
            if w is not None:
                o.deps.add(w)
            for r in self.readers.get(t, ()):
                o.deps.add(r)
        for t in writes:
            self.last_writer[t] = o.idx
            self.readers[t] = []
        for t in reads:
            if t not in writes:
                self.readers.setdefault(t, []).append(o.idx)
        if dma and own_sem:
            self.n_own = getattr(self, "n_own", 0) + 1
            o.semid = ("own", self.n_own)
        elif dma:
            k = self.dma_count % self.N_DMA_SEMS
            self.dma_count += 1
            prev = self.dma_last_on_sem.get(k)
            if prev is not None:
                o.deps.add(prev)
            self.dma_last_on_sem[k] = o.idx
            o.semid = ("dma", k)
        else:
            o.semid = ("eng", eng)
        o.deps.discard(o.idx)
        self.ops.append(o)
        return o

    def emit(self):
        nc = self.nc
        ops = self.ops
        for o in ops:
            for d in o.deps:
                p = ops[d]
                if p.dma:
                    p.signal = True
                elif p.eng == "pe" and o.eng == "pe" and not o.dma:
                    continue
                else:
                    p.signal = True
        counters = {}
        for o in ops:
            if o.dma:
                o.signal = True
            if o.signal:
                inc = 16 if o.dma else 1
                counters[o.semid] = counters.get(o.semid, 0) + inc
                o.semval = counters[o.semid]
        semids = sorted(counters.keys(), key=str)
        with ExitStack() as es:
            sems = {}
            for sid in semids:
                sems[sid] = es.enter_context(nc.semaphore("s_%s_%s" % sid))
            block = es.enter_context(nc.Block())
            per_eng = {e: [o for o in ops if o.eng == e] for e in ENGINES}

            def run(e, handle):
                seen = {}
                cnt = [0]

                class _P:
                    def matmul(self_, *a, **k):
                        cnt[0] += 1
                        return handle.matmul(*a, **k)

                    def transpose(self_, *a, **k):
                        cnt[0] += 1
                        return handle.transpose(*a, **k)
                proxy = _P()
                for o in per_eng[e]:
                    need = {}
                    for d in o.deps:
                        p = ops[d]
                        if not p.signal:
                            continue
                        if (not p.dma) and p.eng == "pe" and e == "pe" and not o.dma:
                            continue
                        if p.semval > need.get(p.semid, 0):
                            need[p.semid] = p.semval
                    for sid, v in need.items():
                        if seen.get(sid, 0) >= v:
                            continue
                        seen[sid] = v
                        handle.wait_ge(sems[sid], v)
                    if e == "pe" and self.count_pe:
                        n0 = cnt[0]
                        ins = o.fn(proxy)
                        self.pe_map.append((o.tag, n0, cnt[0]))
                    else:
                        ins = o.fn(handle)
                    if o.signal:
                        ins.then_inc(sems[o.semid], 16 if o.dma else 1)

            if per_eng["pe"]:
                @block.tensor
                def _(h):
                    run("pe", h)
            if per_eng["act"]:
                @block.scalar
                def _(h):
                    run("act", h)
            if per_eng["dve"]:
                @block.vector
                def _(h):
                    run("dve", h)
            if per_eng["pool"]:
                @block.gpsimd
                def _(h):
                    run("pool", h)
            if per_eng["sp"]:
                @block.sync
                def _(h):
                    run("sp", h)


PV_CW = 0
PV_CB = 32
PV_BA = 40
PV_BX = 48
PV_LAM = 56
PV_SINK = 64
PV_BG = 72
PV_N = 88

SLOPES = [2.0 ** (-8.0 * (i + 1) / 16.0) for i in range(16)]
EW = "pool"


def build_program(nch=NCH, last_stages="FRAMO", ndma=24, debug=False):
    nc = bass.Bass("TRN2", target_bir_lowering=False)
    x = nc.dram_tensor("x", [T, D], F32, kind="ExternalInput").ap()
    wR = nc.dram_tensor("wR", [8, 128, 8, 256], F32, kind="ExternalInput").ap()
    wA = nc.dram_tensor("wA", [4, 128, 8, 704], F32, kind="ExternalInput").ap()
    wM = nc.dram_tensor("wM", [8, 128, 8, 256], F32, kind="ExternalInput").ap()
    wO3 = nc.dram_tensor("wO3", [3, 128, 8, 1024], F32, kind="ExternalInput").ap()
    wG = nc.dram_tensor("wG", [2, 128, 8, 128], F32, kind="ExternalInput").ap()
    pvec = nc.dram_tensor("pvec", [128, PV_N], F32, kind="ExternalInput").ap()
    gbc = nc.dram_tensor("gbc", [2, 128, 1024], F32, kind="ExternalInput").ap()
    dmask_d = nc.dram_tensor("dmask", [128, 256], F32, kind="ExternalInput").ap()
    out = nc.dram_tensor("out", [T, D], F32, kind="ExternalOutput").ap()

    if debug:
        dbg_mrg = nc.dram_tensor("dbg_mrg", [128, 8 * TC], BF16, kind="ExternalOutput").ap()
        dbg_ot = nc.dram_tensor("dbg_ot", [128, 1024], F32, kind="ExternalOutput").ap()
        dbg_sso = nc.dram_tensor("dbg_sso", [128, 16], F32, kind="ExternalOutput").ap()
        dbg_xt = nc.dram_tensor("dbg_xt", [128, 1024], F32, kind="ExternalOutput").ap()
        dbg_wo = nc.dram_tensor("dbg_wo", [128, 8 * 1024], BF16, kind="ExternalOutput").ap()
        dbg_hT = nc.dram_tensor("dbg_hT", [128, 8 * TC], BF16, kind="ExternalOutput").ap()
        dbg_ygr = nc.dram_tensor("dbg_ygr", [128, 8 * TC], BF16, kind="ExternalOutput").ap()
        dbg_yga = nc.dram_tensor("dbg_yga", [128, 8 * TC], BF16, kind="ExternalOutput").ap()
    S = Sched(nc)
    S.N_DMA_SEMS = ndma
    sb = nc.alloc_sbuf_tensor

    wro = sb("wro", [128, 8, 1024], BF16)
    wao = sb("wao", [128, 8, 1024], BF16)
    wo = sb("wo", [128, 8, 1024], BF16)
    wga = sb("wga", [128, 8, 128], BF16)
    wgx = sb("wgx", [128, 8, 128], BF16)
    gpre = sb("gpre", [128, 1024], F32)
    gpost = sb("gpost", [128, 1024], F32)
    dmask = sb("dmask_sb", [128, 256], F32)
    pv = sb("pv", [128, PV_N], F32)
    dc = sb("dc", [128, 64], F32)
    DC_HBA, DC_HBX, DC_S, DC_HS, DC_ESINK, DC_HBG = 0, 8, 16, 24, 32, 40
    ident = sb("ident", [128, 128], BF16)
    identf = sb("identf", [128, 128], F32)
    ones = sb("ones", [128, 64], BF16)
    wslot = [sb("wslot%d" % i, [128, 8 * 448], BF16) for i in range(2)]

    def wv(i, ncols):
        return wslot[i][:, 0:8 * ncols].rearrange("p (d c) -> p d c", d=8)
    xt = [sb("xt%d" % i, [128, 1024], F32) for i in range(2)]
    hTs = [sb("hT%d" % i, [128, 8, TC], BF16) for i in range(2)]
    ygrs = [sb("ygr%d" % i, [128, 8, TC], BF16) for i in range(2)]
    Mt = sb("Mt", [128, 4, TC], F32)
    CH = {"p": 0}
    yga = sb("yga", [128, 8, TC], BF16)
    mrg = sb("mrg", [128, 8, TC], BF16)
    qT = sb("qT", [128, 2, TC], BF16)
    kbuf = [sb("kbuf%d" % g, [128, 128 + TC], BF16) for g in range(4)]
    vbuf = [sb("vbuf%d" % g, [128, 5, 64], BF16) for g in range(4)]
    cstate = sb("cstate", [128, 8, 4], F32)
    hstate = sb("hstate", [128, 8], F32)
    ssf = sb("ssf", [128, 8], F32)
    rstf = sb("rstf", [128, 8], F32)
    sso = sb("sso", [128, 16], F32)
    NG = 26
    G = sb("G", [128, NG, 512], F32)
    xr = [sb("xr%d" % i, [128, 3 + TC], F32) for i in range(2)]

    def g32(i):
        return G[:, i, :]

    def gbf(i):
        return G[:, i, :].bitcast(BF16)

    HB0 = 13

    def hbv(tt):
        return gbf(HB0 + tt)

    PS2 = [nc.alloc_psum_tensor("ps2_%d" % i, [128, 1024], F32) for i in range(4)]

    def bank(b):
        return PS2[b // 2][:, (b % 2) * 512:(b % 2 + 1) * 512]

    def btok(b):
        return ("bank", b)

    dma_w = "pool"

    S.op("sp", lambda e: e.dma_start(out=pv[:], in_=pvec), writes=["pv"], dma=True)
    S.op("sp", lambda e: e.dma_start(out=gpre[:], in_=gbc[0]), writes=["gpre"], dma=True)
    S.op("sp", lambda e: e.dma_start(out=dmask[:], in_=dmask_d), writes=["dmask"], dma=True)
    scrR = nc.dram_tensor("scrR", [8, 128, 8, 256], BF16, kind="Internal").ap()
    scrAq = nc.dram_tensor("scrAq", [4, 128, 8, 448], BF16, kind="Internal").ap()
    scrAg = nc.dram_tensor("scrAg", [4, 128, 8, 256], BF16, kind="Internal").ap()
    scrM = nc.dram_tensor("scrM", [8, 128, 8, 256], BF16, kind="Internal").ap()
    S.op("dve", lambda e: e.memset(identf[:], 0.0), writes=["identf"])
    S.op("pool", lambda e: e.affine_select(out=identf[:], in_=identf[:], pattern=[[-1, 128]],
                                           compare_op=ALU.not_equal, fill=1.0, base=0,
                                           channel_multiplier=1),
         reads=["identf"], writes=["identf"])
    S.op("dve", lambda e: e.tensor_copy(out=ident[:], in_=identf[:]), reads=["identf"], writes=["ident"])
    S.op(dma_w, lambda e: e.dma_start(out=wv(0, 256), in_=wR[0]), writes=[("wslot", 0)], dma=True,
         own_sem=True)
    S.op(dma_w, lambda e: e.dma_start(out=wv(1, 256), in_=wR[1]), writes=[("wslot", 1)], dma=True,
         own_sem=True)
    S.op(dma_w, lambda e: e.dma_start(out=scrR[0:2], in_=wR[0:2]), writes=["scrR0"], dma=True, own_sem=True)
    S.op(dma_w, lambda e: e.dma_start(out=wga[:], in_=wG[0]), writes=["wga"], dma=True, own_sem=True)
    S.op(dma_w, lambda e: e.dma_start(out=wgx[:], in_=wG[1]), writes=["wgx"], dma=True, own_sem=True)

    def late_casts_1():
        S.op(dma_w, lambda e: e.dma_start(out=scrR[2:8], in_=wR[2:8]), writes=["scrR1"], dma=True, own_sem=True)
        S.op(dma_w, lambda e: e.dma_start(out=scrAq, in_=wA[:, :, :, 0:448]), writes=["scrAq"], dma=True, own_sem=True)
        S.op(dma_w, lambda e: e.dma_start(out=scrAg, in_=wA[:, :, :, 448:704]), writes=["scrAg"], dma=True, own_sem=True)

    def late_casts_2():
        S.op(dma_w, lambda e: e.dma_start(out=wro[:], in_=wO3[0]), writes=["wro"], dma=True, own_sem=True)
        S.op(dma_w, lambda e: e.dma_start(out=scrM, in_=wM), writes=["scrM"], dma=True, own_sem=True)
        S.op(dma_w, lambda e: e.dma_start(out=wao[:], in_=wO3[1]), writes=["wao"], dma=True, own_sem=True)
        S.op(dma_w, lambda e: e.dma_start(out=wo[:], in_=wO3[2]), writes=["wo"], dma=True, own_sem=True)
        S.op("sp", lambda e: e.dma_start(out=gpost[:], in_=gbc[1]), writes=["gpost"], dma=True)
    S.op("dve", lambda e: e.memset(ones[:], 1.0), writes=["ones"])
    nhalf = sb("nhalf", [128, 8], F32)
    S.op("dve", lambda e: e.memset(nhalf[:], -0.5), writes=["nhalf"])
    negm = sb("negm", [128, 128], F32)
    S.op("dve", lambda e: e.memset(negm[:], -1.0e5), writes=["negm"])
    for g_ in range(4):
        S.op("dve", (lambda g_=g_: lambda e: e.memset(kbuf[g_][:, 0:128], 0.0))(), writes=[("kprev", g_)])
        S.op("dve", (lambda g_=g_: lambda e: e.memset(vbuf[g_][:, 0, :], 0.0))(), writes=[("vprev", g_)])
    S.op("dve", lambda e: e.memset(cstate[:], 0.0), writes=["cstate"])
    S.op("dve", lambda e: e.memset(hstate[:], 0.0), writes=["hstate"])
    S.op("dve", lambda e: e.tensor_scalar(out=dc[:, DC_HBA:DC_HBA + 16], in0=pv[:, PV_BA:PV_BA + 16],
                                          scalar1=0.5, scalar2=None, op0=ALU.mult),
         reads=["pv"], writes=["dc_hb"])
    S.op("dve", lambda e: e.tensor_scalar(out=dc[:, DC_HBG:DC_HBG + 16], in0=pv[:, PV_BG:PV_BG + 16],
                                          scalar1=0.5, scalar2=None, op0=ALU.mult),
         reads=["pv"], writes=["dc_hbg"])
    S.op("act", lambda e: e.activation(out=dc[:, DC_S:DC_S + 8], in_=pv[:, PV_LAM:PV_LAM + 8],
                                       func=AF.Exp, scale=-1.0),
         reads=["pv"], writes=["dc_s"])
    S.op("act", lambda e: e.activation(out=dc[:, DC_S:DC_S + 8], in_=dc[:, DC_S:DC_S + 8],
                                       func=AF.Ln, bias=1.0, scale=1.0),
         reads=["dc_s"], writes=["dc_s"])
    S.op("dve", lambda e: e.tensor_scalar(out=dc[:, DC_HS:DC_HS + 8], in0=dc[:, DC_S:DC_S + 8],
                                          scalar1=-4.0, scalar2=None, op0=ALU.mult),
         reads=["dc_s"], writes=["dc_hs"])
    S.op("dve", lambda e: e.tensor_scalar(out=dc[:, DC_S:DC_S + 8], in0=dc[:, DC_S:DC_S + 8],
                                          scalar1=-8.0, scalar2=None, op0=ALU.mult),
         reads=["dc_s", "dc_hs"], writes=["dc_s"])
    S.op("act", lambda e: e.activation(out=dc[:, DC_ESINK:DC_ESINK + 8], in_=pv[:, PV_SINK:PV_SINK + 8],
                                       func=AF.Exp),
         reads=["pv"], writes=["dc_esink"])
    S.op("dve", lambda e: e.tensor_scalar(out=dc[:, 56:64], in0=dc[:, DC_ESINK:DC_ESINK + 8],
                                          scalar1=2.0, scalar2=None, op0=ALU.mult),
         reads=["dc_esink"], writes=["dc_esink2"])

    piece_seq = []
    _units = [(kind, m) for m in range(8) for kind in "LRA"]
    _cost = {"L": 4.2, "R": 2.1, "A": 2.1}
    _rk = {k: (4.2 if k == -2 else 4.7 if k <= 5 else 0.5 if k == 6 else 0.0) for k in range(-2, 9)}
    _budget = (sum(_rk.values()) + sum(_cost[u[0]] for u in _units)) / 11.0
    M_SCHED = {}
    _ui = 0
    for _k in range(-2, 9):
        load = _rk[_k]
        M_SCHED[_k] = []
        while _ui < len(_units) and (_k == 8 or load + 0.5 * _cost[_units[_ui][0]] <= _budget):
            M_SCHED[_k].append(_units[_ui])
            load += _cost[_units[_ui][0]]
            _ui += 1

    def _rp(c):
        return (scrR[c], 256, "scrR0" if c < 2 else "scrR1")
    for _ch in range(nch):
        for _k in range(-2, 9):
            if 0 <= _k + 2 < 8:
                piece_seq.append(_rp(_k + 2))
            if _ch > 0:
                for _kind, _m in M_SCHED[_k]:
                    if _kind == "L":
                        piece_seq.append((scrM[_m], 256, "scrM"))
        for _g in range(4):
            piece_seq.append((scrAq[_g], 448, "scrAq"))
            piece_seq.append((scrAg[_g], 256, "scrAg"))
    piece_seq += [(scrM[m], 256, "scrM") for m in range(8)]
    piece_state = {"issued": 2, "used": 0}

    def issue_piece():
        n = piece_state["issued"]
        if n >= len(piece_seq):
            return
        piece_state["issued"] += 1
        src_ap, ncols, stok = piece_seq[n]
        i = n % 2
        S.op("sp", lambda e: e.dma_start(out=wv(i, ncols), in_=src_ap),
             reads=[stok], writes=[("wslot", i)], dma=True)

    def load_piece(src_ap, ncols):
        n = piece_state["used"]
        piece_state["used"] += 1
        while piece_state["issued"] <= n:
            issue_piece()
        if piece_state["issued"] <= n + 1:
            issue_piece()
        return n % 2

    big_loaded = [False]

    def load_big():
        if big_loaded[0]:
            return
        big_loaded[0] = True
        pass

    JUNK = 12

    def F_pre(ch, parts=(0, 1, 2)):
        T0 = ch * TC
        rp = (ch % 2) * 4

        def ld(tt):
            r0 = T0 + tt * 128
            S.op("sp", (lambda tt=tt, r0=r0: lambda e: e.dma_start(out=xt[tt % 2][:], in_=x[r0:r0 + 128, :]))(),
                 writes=[("xt", tt % 2)], dma=True)

        def proc(tt):
            sl = tt % 2
            col = rp + tt
            S.op("act", (lambda sl=sl, col=col: lambda e: e.activation(
                out=gbf(JUNK), in_=xt[sl][:], func=AF.Square, accum_out=ssf[:, col:col + 1]))(),
                 reads=[("xt", sl)], writes=[("G", JUNK), ("ssf", col)])
            S.op("pool", (lambda col=col: lambda e: e.tensor_scalar(
                out=rstf[:, col:col + 1], in0=ssf[:, col:col + 1], scalar1=1.0 / D, scalar2=EPS,
                op0=ALU.mult, op1=ALU.add))(),
                 reads=[("ssf", col)], writes=[("rstf", col)])
            S.op("pool", (lambda col=col: lambda e: e.tensor_tensor(
                out=rstf[:, col:col + 1], in0=rstf[:, col:col + 1], in1=nhalf[:, 0:1], op=ALU.pow))(),
                 reads=[("rstf", col), "nhalf"], writes=[("rstf", col)])
            S.op("dve", (lambda tt=tt, sl=sl, col=col: lambda e: e.scalar_tensor_tensor(
                out=hbv(tt), in0=xt[sl][:], scalar=rstf[:, col:col + 1], in1=gpre[:],
                op0=ALU.mult, op1=ALU.mult))(),
                 reads=[("xt", sl), ("rstf", col), "gpre"], writes=[("G", HB0 + tt)])
        if 0 in parts:
            ld(0)
            ld(1)
        if 1 in parts:
            proc(0)
            proc(1)
            ld(2)
            ld(3)
        if 2 in parts:
            proc(2)
            proc(3)

    def F_tr(ch):
        for tt in range(4):
            pb = tt % 2

            def tr(e, tt=tt, pb=pb):
                pbv = bank(pb).bitcast(BF16)
                ins = None
                for d in range(8):
                    ins = e.transpose(pbv[:, d * 128:(d + 1) * 128], hbv(tt)[:, d * 128:(d + 1) * 128], ident[:])
                return ins
            S.op("pe", tr, reads=[("G", HB0 + tt), "ident"], writes=[btok(pb)])
            S.op("act", (lambda tt=tt, pb=pb, hTc=hTs[ch % 2]: lambda e: e.activation(
                out=hTc[:, :, tt * 128:(tt + 1) * 128],
                in_=bank(pb).bitcast(BF16).rearrange("p (d t) -> p d t", d=8),
                func=AF.Identity))(),
                 reads=[btok(pb)], writes=[("hT", ch % 2)])

    def R_tiles(c):
        par = c % 2
        gb = par * 13
        return par, gb, par * 4

    def RB_G(c):
        return 2 + (c % 2)

    def RB_X(c):
        return c % 2
    RB_R, RB_I = 4, 5
    MB = [6, 7]

    def R0(c):
        ws = load_piece(wR[c], 256)

        p = CH["p"]

        def mm(col0, b, ws=ws, hTc=hTs[p]):
            def f(e):
                ins = None
                for d in range(8):
                    ins = e.matmul(bank(b), lhsT=wv(ws, 256)[:, d, col0:col0 + 128], rhs=hTc[:, d, :],
                                   start=(d == 0), stop=(d == 7))
                return ins
            return f
        S.op("pe", mm(0, RB_X(c)), reads=[("wslot", ws), ("hT", p)], writes=[btok(RB_X(c))])
        S.op("pe", mm(128, RB_G(c)), reads=[("wslot", ws), ("hT", p)], writes=[btok(RB_G(c))])

    def R1(c):
        par, gb, _pb = R_tiles(c)
        pbase = RB_X(c)
        xr_ = xr[par]
        cv = gb + 0

        S.op(EW, (lambda c=c, xr_=xr_: lambda e: e.tensor_copy(out=xr_[:, 0:3], in_=cstate[:, c, 0:3]))(),
             reads=["cstate"], writes=[("xrh", par)])
        S.op("act", (lambda xr_=xr_, pbase=pbase: lambda e: e.activation(
            out=xr_[:, 3:3 + TC], in_=bank(pbase), func=AF.Identity))(),
             reads=[btok(pbase)], writes=[("xr", par)])
        S.op(EW, (lambda c=c, xr_=xr_: lambda e: e.tensor_copy(out=cstate[:, c, 0:3], in_=xr_[:, TC:TC + 3]))(),
             reads=[("xr", par)], writes=["cstate"])
        S.op("dve", (lambda c=c, xr_=xr_, cv=cv: lambda e: e.tensor_scalar(
            out=g32(cv), in0=xr_[:, 3:3 + TC], scalar1=pv[:, PV_CW + c * 4 + 3:PV_CW + c * 4 + 4],
            scalar2=pv[:, PV_CB + c:PV_CB + c + 1], op0=ALU.mult, op1=ALU.add))(),
             reads=[("xr", par), "pv"], writes=[("G", cv)])
    def R1b(c):
        par, gb, pbase = R_tiles(c)
        xr_ = xr[par]
        cv = gb + 0
        for k in (2, 1, 0):
            S.op("dve", (lambda c=c, xr_=xr_, cv=cv, k=k: lambda e: e.scalar_tensor_tensor(
                out=g32(cv), in0=xr_[:, k:k + TC], scalar=pv[:, PV_CW + c * 4 + k:PV_CW + c * 4 + k + 1],
                in1=g32(cv), op0=ALU.mult, op1=ALU.add))(),
                 reads=[("xr", par), ("xrh", par), "pv", ("G", cv)], writes=[("G", cv)])

    def R2(c):
        par, gb, _pb = R_tiles(c)
        pb1, pb2, pb3 = RB_G(c), RB_R, RB_I
        cv, tr_, ti_, a_, e2_, hm_, w_, u_, y_, tg_, tmp2_, cvb_i = [gb + k for k in range(12)]
        S.op("act", (lambda cv=cv, cvb_i=cvb_i: lambda e: e.activation(
            out=gbf(cvb_i)[:, 0:TC], in_=g32(cv), func=AF.Identity))(),
             reads=[("G", cv)], writes=[("G", cvb_i)])
        S.op("pe", (lambda c=c, cvb_i=cvb_i, pb1=pb1, pb2=pb2, pb3=pb3: lambda e: e.matmul(
            bank(pb2), lhsT=wga[:, c, :], rhs=gbf(cvb_i)[:, 0:TC], start=True, stop=True))(),
             reads=["wga", ("G", cvb_i)], writes=[btok(pb2)])
        S.op("pe", (lambda c=c, cvb_i=cvb_i, pb1=pb1, pb2=pb2, pb3=pb3: lambda e: e.matmul(
            bank(pb3), lhsT=wgx[:, c, :], rhs=gbf(cvb_i)[:, 0:TC], start=True, stop=True))(),
             reads=["wgx", ("G", cvb_i)], writes=[btok(pb3)])
        S.op("act", (lambda c=c, tr_=tr_, pb1=pb1, pb2=pb2, pb3=pb3: lambda e: e.activation(
            out=g32(tr_), in_=bank(pb2), func=AF.Tanh, scale=0.5,
            bias=dc[:, DC_HBA + c:DC_HBA + c + 1]))(),
             reads=[btok(pb2), "dc_hb"], writes=[("G", tr_)])
        S.op("act", (lambda c=c, ti_=ti_, pb1=pb1, pb2=pb2, pb3=pb3: lambda e: e.activation(
            out=g32(ti_), in_=bank(pb3), func=AF.Tanh, scale=0.5,
            bias=dc[:, DC_HBX + c:DC_HBX + c + 1]))(),
             reads=[btok(pb3), "dc_hb"], writes=[("G", ti_)])
        S.op("act", (lambda c=c, tr_=tr_, a_=a_: lambda e: e.activation(
            out=g32(a_), in_=g32(tr_), func=AF.Exp, scale=dc[:, DC_HS + c:DC_HS + c + 1],
            bias=dc[:, DC_HS + c:DC_HS + c + 1]))(),
             reads=[("G", tr_), "dc_hs"], writes=[("G", a_)])
        S.op("act", (lambda c=c, tr_=tr_, e2_=e2_: lambda e: e.activation(
            out=g32(e2_), in_=g32(tr_), func=AF.Exp, scale=dc[:, DC_S + c:DC_S + c + 1],
            bias=dc[:, DC_S + c:DC_S + c + 1]))(),
             reads=[("G", tr_), "dc_s"], writes=[("G", e2_)])
        S.op("act", (lambda tg_=tg_, pb1=pb1, pb2=pb2, pb3=pb3: lambda e: e.activation(
            out=g32(tg_), in_=bank(pb1), func=AF.Tanh, scale=0.5))(),
             reads=[btok(pb1)], writes=[("G", tg_)])
        S.op("dve", (lambda ti_=ti_, cv=cv, w_=w_: lambda e: e.scalar_tensor_tensor(
            out=g32(w_), in0=g32(ti_), scalar=1.0, in1=g32(cv), op0=ALU.add, op1=ALU.mult))(),
             reads=[("G", ti_), ("G", cv)], writes=[("G", w_)])
        S.op("dve", (lambda tg_=tg_, tmp2_=tmp2_, pb1=pb1, pb2=pb2, pb3=pb3: lambda e: e.scalar_tensor_tensor(
            out=g32(tmp2_), in0=g32(tg_), scalar=1.0, in1=bank(pb1), op0=ALU.add, op1=ALU.mult))(),
             reads=[("G", tg_), btok(pb1)], writes=[("G", tmp2_)])

    def R3a(c):
        par, gb, pbase = R_tiles(c)
        e2_, hm_ = gb + 4, gb + 5
        S.op("act", (lambda e2_=e2_, hm_=hm_: lambda e: e.activation(
            out=g32(hm_), in_=g32(e2_), func=AF.Sqrt, scale=-0.25, bias=0.25))(),
             reads=[("G", e2_)], writes=[("G", hm_)])

    def R3b(c):
        par, gb, pbase = R_tiles(c)
        cv, tr_, ti_, a_, e2_, hm_, w_, u_, y_, tg_, tmp2_, cvb_i = [gb + k for k in range(12)]
        S.op("dve", (lambda hm_=hm_, w_=w_, u_=u_: lambda e: e.tensor_tensor(
            out=g32(u_), in0=g32(hm_), in1=g32(w_), op=ALU.mult))(),
             reads=[("G", hm_), ("G", w_)], writes=[("G", u_)])
        S.op("dve", (lambda c=c, a_=a_, u_=u_, y_=y_: lambda e: e.tensor_tensor_scan(
            out=g32(y_), data0=g32(a_), data1=g32(u_), initial=hstate[:, c:c + 1],
            op0=ALU.mult, op1=ALU.add))(),
             reads=[("G", a_), ("G", u_), "hstate"], writes=[("G", y_)])
        S.op(EW, (lambda c=c, y_=y_: lambda e: e.tensor_copy(out=hstate[:, c:c + 1], in_=g32(y_)[:, TC - 1:TC]))(),
             reads=[("G", y_)], writes=["hstate"])
        S.op("dve", (lambda c=c, tmp2_=tmp2_, y_=y_, yg=ygrs[CH["p"]]: lambda e: e.scalar_tensor_tensor(
            out=yg[:, c, :], in0=g32(tmp2_), scalar=0.5, in1=g32(y_), op0=ALU.mult, op1=ALU.mult))(),
             reads=[("G", tmp2_), ("G", y_)], writes=[("ygr", CH["p"], c)])

    A_SC = [0, 1]
    A_SG = [[2, 3], [4, 5]]
    A_TGA, A_DS, A_YA = 6, 17, 25
    PT0 = 7
    a_rot = [0]
    a_ws = {}

    def A_P(g):
        p = CH["p"]
        hTc = hTs[p]
        ws = load_piece(None, 448)
        kb_, vb_ = kbuf[g], vbuf[g]

        def nextbank():
            b = a_rot[0] % 2
            a_rot[0] += 1
            return b

        def proj(col0, b, ws, nc_=448):
            def f(e, ws=ws, hTc=hTc, nc_=nc_):
                ins = None
                for d in range(8):
                    ins = e.matmul(bank(b), lhsT=wv(ws, nc_)[:, d, col0:col0 + 128], rhs=hTc[:, d, :],
                                   start=(d == 0), stop=(d == 7))
                return ins
            return f
        for j in range(2):
            b = nextbank()
            S.op("pe", proj(j * 128, b, ws), reads=[("wslot", ws), ("hT", p)], writes=[btok(b)])
            S.op("act", (lambda j=j, b=b: lambda e: e.activation(
                out=qT[:, j, :], in_=bank(b), func=AF.Identity, scale=0.125))(),
                 reads=[btok(b)], writes=[("qT", j)])
        b = nextbank()
        S.op("pe", proj(256, b, ws), reads=[("wslot", ws), ("hT", p)], writes=[btok(b)])
        S.op("act", (lambda kb_=kb_, b=b: lambda e: e.activation(
            out=kb_[:, 128:128 + TC], in_=bank(b), func=AF.Identity))(),
             reads=[btok(b)], writes=[("kcur", g)])
        b = nextbank()

        def mm_v(e, ws=ws, b=b, hTc=hTc):
            ins = None
            for tt in range(4):
                for d in range(8):
                    ins = e.matmul(bank(b)[:, tt * 64:(tt + 1) * 64], lhsT=hTc[:, d, tt * 128:(tt + 1) * 128],
                                   rhs=wv(ws, 448)[:, d, 384:448], start=(d == 0), stop=(d == 7))
            return ins
        S.op("pe", mm_v, reads=[("wslot", ws), ("hT", p)], writes=[btok(b)])
        S.op("act", (lambda vb_=vb_, b=b: lambda e: e.activation(
            out=vb_[:, 1:5, :], in_=bank(b)[:, 0:256].rearrange("p (t d) -> p t d", t=4),
            func=AF.Identity))(),
             reads=[btok(b)], writes=[("vcur", g)])
        ws2 = load_piece(None, 256)
        for j in range(2):
            b = nextbank()
            sg = A_SG[g % 2][j]
            S.op("pe", proj(j * 128, b, ws2, 256), reads=[("wslot", ws2), ("hT", p)], writes=[btok(b)])
            S.op("act", (lambda b=b: lambda e: e.activation(
                out=g32(A_TGA), in_=bank(b), func=AF.Tanh, scale=0.5))(),
                 reads=[btok(b)], writes=[("G", A_TGA)])
            S.op("dve", (lambda sg=sg, b=b: lambda e: e.scalar_tensor_tensor(
                out=g32(sg), in0=g32(A_TGA), scalar=1.0, in1=bank(b), op0=ALU.add, op1=ALU.mult))(),
                 reads=[("G", A_TGA), btok(b)], writes=[("G", sg)])

    def A_S(g, first):
        kb_ = kbuf[g]
        srot = 0
        for kb in range(5):
            if kb == 0:
                q0, w_, msk = 0, 128, (negm[:, 0:128] if first else dmask[:, 128:256])
            elif kb == 4:
                q0, w_, msk = 384, 128, dmask[:, 0:128]
            else:
                q0, w_, msk = (kb - 1) * 128, 256, dmask[:, 0:256]
            for ph in range(2):
                sbk = 2 + (srot % 2)
                sci = A_SC[srot % 2]
                srot += 1
                pr = slice(ph * 64, (ph + 1) * 64)
                S.op("pe", (lambda kb=kb, pr=pr, q0=q0, w_=w_, sbk=sbk, kb_=kb_: lambda e: e.matmul(
                    bank(sbk)[:, 0:2 * w_], lhsT=kb_[pr, kb * 128:(kb + 1) * 128], rhs=qT[pr, :, q0:q0 + w_],
                    start=True, stop=True))(),
                     reads=[("kcur", g), ("kprev", g), ("qT", 0), ("qT", 1)], writes=[btok(sbk)])
                for j in range(2):
                    head = 4 * g + 2 * j + ph
                    S.op("dve", (lambda j=j, head=head, w_=w_, sbk=sbk, sci=sci, msk=msk: lambda e: e.scalar_tensor_tensor(
                        out=g32(sci)[:, j * w_:(j + 1) * w_], in0=msk, scalar=float(SLOPES[head]),
                        in1=bank(sbk)[:, j * w_:(j + 1) * w_], op0=ALU.mult, op1=ALU.add))(),
                         reads=["dmask", "negm", btok(sbk)], writes=[("G", sci)])
                S.op("act", (lambda kb=kb, ph=ph, w_=w_, sci=sci: lambda e: e.activation(
                    out=gbf(PT0 + kb)[:, ph * 512:ph * 512 + 2 * w_], in_=g32(sci)[:, 0:2 * w_], func=AF.Exp))(),
                     reads=[("G", sci)], writes=[("PT", kb, ph), ("G", PT0 + kb)])

    def A_V(g):
        vb_ = vbuf[g]
        kb_ = kbuf[g]
        Ot, Dt = PS2[2], PS2[3]

        def ptview(kb, ph, part):
            base = gbf(PT0 + kb)[:, ph * 512:(ph + 1) * 512]
            if kb == 0 or kb == 4:
                return base[:, 0:256]
            return base.rearrange("p (j q) -> p j q", j=2)[:, :, part * 128:(part + 1) * 128]

        def mm_pv(e):
            ins = None
            for i in range(4):
                for ph in range(2):
                    pr = slice(ph * 64, (ph + 1) * 64)
                    terms = [(i, 1), (i + 1, 0)]
                    for ti, (kb, part) in enumerate(terms):
                        e.matmul(Ot[pr, i * 256:(i + 1) * 256], lhsT=vb_[:, kb, :], rhs=ptview(kb, ph, part),
                                 start=(ti == 0), stop=(ti == 1))
                    for ti, (kb, part) in enumerate(terms):
                        ins = e.matmul(Dt[pr, i * 256:(i + 1) * 256], lhsT=ones[:, 0:64], rhs=ptview(kb, ph, part),
                                       start=(ti == 0), stop=(ti == 1))
            return ins
        S.op("pe", mm_pv,
             reads=[("vcur", g), ("vprev", g), "ones"] + [("PT", kb, ph) for kb in range(5) for ph in range(2)]
             + [("G", PT0 + kb) for kb in range(5)],
             writes=[btok(4), btok(5), btok(6), btok(7)])
    def A_N(g):
        vb_ = vbuf[g]
        kb_ = kbuf[g]
        Ot, Dt = PS2[2], PS2[3]
        Ov = Ot[:].rearrange("p (i j q) -> p i j q", i=4, j=2)
        Dv = Dt[:].rearrange("p (i j q) -> p i j q", i=4, j=2)
        DSs = [17, 18]
        YAs = [25, 19]
        for j in range(2):
            cc = 2 * g + j
            S.op("act", (lambda j=j, cc=cc: lambda e: e.activation(
                out=g32(DSs[j]).rearrange("p (i q) -> p i q", i=4), in_=Dv[:, :, j, :], func=AF.Ln,
                scale=2.0, bias=dc[:, 56 + cc:56 + cc + 1]))(),
                 reads=[btok(6), btok(7), "dc_esink2"], writes=[("G", DSs[j])])
        for j in range(2):
            S.op("act", (lambda j=j: lambda e: e.activation(
                out=g32(DSs[j]), in_=g32(DSs[j]), func=AF.Exp, scale=-1.0))(),
                 reads=[("G", DSs[j])], writes=[("G", DSs[j])])
        for j in range(2):
            S.op("dve", (lambda j=j: lambda e: e.tensor_tensor(
                out=g32(YAs[j]).rearrange("p (i q) -> p i q", i=4), in0=Ov[:, :, j, :],
                in1=g32(DSs[j]).rearrange("p (i q) -> p i q", i=4), op=ALU.mult))(),
                 reads=[btok(4), btok(5), ("G", DSs[j])], writes=[("G", YAs[j])])
        for j in range(2):
            cc = 2 * g + j
            sg = A_SG[g % 2][j]
            S.op("dve", (lambda j=j, cc=cc, sg=sg: lambda e: e.tensor_tensor(
                out=yga[:, cc, :], in0=g32(YAs[j]), in1=g32(sg), op=ALU.mult))(),
                 reads=[("G", YAs[j]), ("G", sg)], writes=[("yga", cc)])
        S.op(EW, (lambda kb_=kb_: lambda e: e.tensor_copy(out=kb_[:, 0:128], in_=kb_[:, TC:TC + 128]))(),
             reads=[("kcur", g)], writes=[("kprev", g)])
        S.op(EW, (lambda vb_=vb_: lambda e: e.tensor_copy(out=vb_[:, 0, :], in_=vb_[:, 4, :]))(),
             reads=[("vcur", g)], writes=[("vprev", g)])

    m_ws = {}

    def M_a(m, p, part="LR"):
        par = m % 2
        ws = load_piece(None, 256) if "L" in part else None
        hTc, ygc = hTs[p], ygrs[p]
        t_r, t_a = Mt[:, par * 2 + 0, :], Mt[:, par * 2 + 1, :]
        tk_r, tk_a = ("Mt", par, 0), ("Mt", par, 1)

        def mm_l(col0, b, ws=ws, hTc=hTc):
            def f(e):
                ins = None
                for d in range(8):
                    ins = e.matmul(bank(b), lhsT=wv(ws, 256)[:, d, col0:col0 + 128], rhs=hTc[:, d, :],
                                   start=(d == 0), stop=(d == 7))
                return ins
            return f

        def mm_b(wt, src, b, m=m):
            def f(e):
                ins = None
                for c in range(8):
                    ins = e.matmul(bank(b), lhsT=wt[:, c, m * 128:(m + 1) * 128], rhs=src[:, c, :],
                                   start=(c == 0), stop=(c == 7))
                return ins
            return f
        if "L" in part:
            S.op("pe", mm_l(0, MB[0]), reads=[("wslot", ws), ("hT", p)], writes=[btok(MB[0])])
            S.op("pe", mm_l(128, MB[1]), reads=[("wslot", ws), ("hT", p)], writes=[btok(MB[1])])
            S.op("act", (lambda m=m, t_r=t_r: lambda e: e.activation(
                out=t_r, in_=bank(MB[0]), func=AF.Tanh, scale=0.5, bias=dc[:, DC_HBG + m:DC_HBG + m + 1]))(),
                 reads=[btok(MB[0]), "dc_hbg"], writes=[tk_r])
            S.op("act", (lambda m=m, t_a=t_a: lambda e: e.activation(
                out=t_a, in_=bank(MB[1]), func=AF.Tanh, scale=0.5,
                bias=dc[:, DC_HBG + 8 + m:DC_HBG + 8 + m + 1]))(),
                 reads=[btok(MB[1]), "dc_hbg"], writes=[tk_a])
        if "R" not in part:
            return
        S.op("pe", mm_b(wro, ygc, MB[0]), reads=["wro"] + [("ygr", p, c) for c in range(8)],
             writes=[btok(MB[0])])
        S.op("dve", (lambda t_r=t_r: lambda e: e.scalar_tensor_tensor(
            out=t_r, in0=t_r, scalar=1.0, in1=bank(MB[0]), op0=ALU.add, op1=ALU.mult))(),
             reads=[tk_r, btok(MB[0])], writes=[tk_r])

    def M_b(m, p):
        par = m % 2
        t_r, t_a = Mt[:, par * 2 + 0, :], Mt[:, par * 2 + 1, :]
        tk_r, tk_a = ("Mt", par, 0), ("Mt", par, 1)

        def mm_b(wt, src, b, m=m):
            def f(e):
                ins = None
                for c in range(8):
                    ins = e.matmul(bank(b), lhsT=wt[:, c, m * 128:(m + 1) * 128], rhs=src[:, c, :],
                                   start=(c == 0), stop=(c == 7))
                return ins
            return f
        S.op("pe", mm_b(wao, yga, MB[1]), reads=["wao"] + [("yga", c) for c in range(8)],
             writes=[btok(MB[1])])
        S.op("dve", (lambda t_a=t_a: lambda e: e.scalar_tensor_tensor(
            out=t_a, in0=t_a, scalar=1.0, in1=bank(MB[1]), op0=ALU.add, op1=ALU.mult))(),
             reads=[tk_a, btok(MB[1])], writes=[tk_a])
        S.op("dve", (lambda m=m, t_r=t_r, t_a=t_a: lambda e: e.tensor_tensor(
            out=mrg[:, m, :], in0=t_r, in1=t_a, op=ALU.add))(),
             reads=[tk_r, tk_a], writes=[("mrg", m)])

    OT0 = 14

    def O_a(ch, tt):
        pbase = 4 + (tt % 2) * 2
        for hf in range(2):
            def mm_o(e, tt=tt, hf=hf, pbase=pbase):
                ins = None
                for m in range(8):
                    ins = e.matmul(bank(pbase + hf), lhsT=mrg[:, m, tt * 128:(tt + 1) * 128],
                                   rhs=wo[:, m, hf * 512:(hf + 1) * 512], start=(m == 0), stop=(m == 7))
                return ins
            S.op("pe", mm_o, reads=["wo"] + [("mrg", m) for m in range(8)], writes=[btok(pbase + hf)])
            S.op("act", (lambda tt=tt, hf=hf, pbase=pbase: lambda e: e.activation(
                out=gbf(JUNK)[:, 0:512], in_=bank(pbase + hf), func=AF.Square,
                accum_out=sso[:, tt * 2 + hf:tt * 2 + hf + 1]))(),
                 reads=[btok(pbase + hf)], writes=[("G", JUNK), ("sso", tt, hf)])
        S.op("dve", (lambda tt=tt: lambda e: e.tensor_tensor(
            out=sso[:, 8 + tt:9 + tt], in0=sso[:, tt * 2:tt * 2 + 1], in1=sso[:, tt * 2 + 1:tt * 2 + 2],
            op=ALU.add))(),
             reads=[("sso", tt, 0), ("sso", tt, 1)], writes=[("ssr", tt)])
        S.op("pool", (lambda tt=tt: lambda e: e.tensor_scalar(
            out=sso[:, 8 + tt:9 + tt], in0=sso[:, 8 + tt:9 + tt], scalar1=1.0 / D, scalar2=4.0 * EPS,
            op0=ALU.mult, op1=ALU.add))(),
             reads=[("ssr", tt)], writes=[("ssr", tt)])
        S.op("pool", (lambda tt=tt: lambda e: e.tensor_tensor(
            out=sso[:, 8 + tt:9 + tt], in0=sso[:, 8 + tt:9 + tt], in1=nhalf[:, 0:1], op=ALU.pow))(),
             reads=[("ssr", tt), "nhalf"], writes=[("ssr", tt)])

    def O_b1(ch, tt):
        pbase = 4 + (tt % 2) * 2
        ot = G[:, OT0 + 2 * tt:OT0 + 2 * tt + 2, :]
        ottok = [("G", OT0 + 2 * tt), ("G", OT0 + 2 * tt + 1)]
        for hf in range(2):
            S.op("dve", (lambda tt=tt, hf=hf, pbase=pbase, ot=ot: lambda e: e.scalar_tensor_tensor(
                out=ot[:, hf, :], in0=bank(pbase + hf), scalar=sso[:, 8 + tt:9 + tt],
                in1=gpost[:, hf * 512:(hf + 1) * 512], op0=ALU.mult, op1=ALU.mult))(),
                 reads=[btok(pbase + hf), ("ssr", tt), "gpost"], writes=[ottok[hf]])

    def O_ld(ch, tt):
        r0 = ch * TC + tt * 128
        par = tt % 2
        xr_ap = G[:, 22 + 2 * par:24 + 2 * par, :].rearrange("p a b -> p (a b)")
        S.op("sp", (lambda r0=r0, xr_ap=xr_ap: lambda e: e.dma_start(out=xr_ap, in_=x[r0:r0 + 128, :]))(),
             writes=[("G", 22 + 2 * par), ("G", 23 + 2 * par)], dma=True)

    def O_b2(ch, tt):
        r0 = ch * TC + tt * 128
        par = tt % 2
        otf = G[:, OT0 + 2 * tt:OT0 + 2 * tt + 2, :].rearrange("p a b -> p (a b)")
        ottok = [("G", OT0 + 2 * tt), ("G", OT0 + 2 * tt + 1)]
        xr_ap = G[:, 22 + 2 * par:24 + 2 * par, :].rearrange("p a b -> p (a b)")
        S.op("dve", (lambda otf=otf, xr_ap=xr_ap: lambda e: e.tensor_tensor(out=otf, in0=otf, in1=xr_ap, op=ALU.add))(),
             reads=ottok + [("G", 22 + 2 * par), ("G", 23 + 2 * par)], writes=ottok)
        S.op("sp", (lambda r0=r0, otf=otf: lambda e: e.dma_start(out=out[r0:r0 + 128, :], in_=otf))(),
             reads=ottok, dma=True)

    def ph(name, f, *a):
        S.stage = name
        f(*a)

    def R_iter(k):
        if 0 <= k + 2 < 8:
            ph("R0(%d)" % (k + 2), R0, k + 2)
            ph("R1(%d)" % (k + 2), R1, k + 2)
            ph("R1b(%d)" % (k + 2), R1b, k + 2)
        if k % 2 == 0:
            pair = [cc for cc in (k - 1, k) if 0 <= cc < 8]
            for cc in pair:
                ph("R3a(%d)" % cc, R3a, cc)
            for cc in pair:
                ph("R3b(%d)" % cc, R3b, cc)
        if 0 <= k + 1 < 8:
            ph("R2(%d)" % (k + 1), R2, k + 1)

    def O_stage(ch):
        ph("Oa(0)", O_a, ch, 0)
        ph("Oa(1)", O_a, ch, 1)
        ph("Old", O_ld, ch, 0)
        ph("Old", O_ld, ch, 1)
        ph("Ob1(0)", O_b1, ch, 0)
        ph("Ob1(1)", O_b1, ch, 1)
        ph("Oa(2)", O_a, ch, 2)
        ph("Oa(3)", O_a, ch, 3)
        ph("Ob2(0)", O_b2, ch, 0)
        ph("Ob2(1)", O_b2, ch, 1)
        ph("Old", O_ld, ch, 2)
        ph("Old", O_ld, ch, 3)
        ph("Ob1(2)", O_b1, ch, 2)
        ph("Ob1(3)", O_b1, ch, 3)
        ph("Ob2(2)", O_b2, ch, 2)
        ph("Ob2(3)", O_b2, ch, 3)

    ph("Fpre", F_pre, 0)
    late_casts_1()
    late_casts_2()
    ph("Ftr", F_tr, 0)
    for ch in range(nch):
        first = ch == 0
        CH["p"] = ch % 2
        for k in range(-2, 9):
            R_iter(k)
            if ch > 0:
                for kind, m in M_SCHED[k]:
                    if kind == "A":
                        ph("Mb(%d)" % m, M_b, m, (ch - 1) % 2)
                    else:
                        ph("M%s(%d)" % (kind, m), M_a, m, (ch - 1) % 2, kind)
        po = ch - 1
        if ch > 0:
            ph("Oa(0)", O_a, po, 0)
            ph("Oa(1)", O_a, po, 1)
            ph("Old", O_ld, po, 0)
            ph("Old", O_ld, po, 1)
        ph("AP(0)", A_P, 0)
        if ch > 0:
            ph("Ob1(0)", O_b1, po, 0)
            ph("Ob1(1)", O_b1, po, 1)
            ph("Oa(2)", O_a, po, 2)
            ph("Oa(3)", O_a, po, 3)
        ph("AS(0)", A_S, 0, first)
        if ch > 0:
            ph("Ob2(0)", O_b2, po, 0)
            ph("Ob2(1)", O_b2, po, 1)
            ph("Old", O_ld, po, 2)
            ph("Old", O_ld, po, 3)
            ph("Ob1(2)", O_b1, po, 2)
            ph("Ob1(3)", O_b1, po, 3)
        for g in range(4):
            if g + 1 < 4:
                ph("AP(%d)" % (g + 1), A_P, g + 1)
            if g == 0 and ch > 0:
                ph("Ob2(2)", O_b2, po, 2)
                ph("Ob2(3)", O_b2, po, 3)
            ph("AV(%d)" % g, A_V, g)
            if g + 1 < 4:
                ph("AS(%d)" % (g + 1), A_S, g + 1, first)
            if g == 3 and ch + 1 < nch:
                ph("Ftr", F_tr, ch + 1)
            ph("AN(%d)" % g, A_N, g)
            if g < 3 and ch + 1 < nch:
                ph("Fpre%d" % g, F_pre, ch + 1, (g,))
    for m in range(8):
        ph("Ma(%d)" % m, M_a, m, (nch - 1) % 2)
        ph("Mb(%d)" % m, M_b, m, (nch - 1) % 2)
    O_stage(nch - 1)

    stores = [o.idx for o in S.ops if o.dma]
    fin = S.op("sp", lambda e: e.nop())
    fin.deps.update(stores)
    S.emit()
    return nc


def _layouts(pre_norm_g, w_in, b_gate, conv_w, conv_b, w_rg_a, b_rg_a, w_rg_x, b_rg_x,
             lru_lambda, attn_sinks, w_rnn_out, w_attn_out, w_out, post_norm_g):
    f = np.float32
    W = np.ascontiguousarray(w_in[0], dtype=f)
    Wd = W.reshape(8, 128, 6656)
    o_rx, o_rg, o_q, o_k, o_v, o_ag, o_ml = 0, 1024, 2048, 3072, 3328, 3584, 4608

    def cols(c0, n):
        return Wd[:, :, c0:c0 + n].transpose(1, 0, 2)

    wR = np.stack([np.concatenate([cols(o_rx + c * 128, 128), cols(o_rg + c * 128, 128)], axis=2)
                   for c in range(8)]).astype(f)
    wA = np.stack([np.concatenate([cols(o_q + g * 256, 256), cols(o_k + g * 64, 64), cols(o_k + g * 64, 64),
                                   cols(o_v + g * 64, 64), cols(o_ag + g * 256, 256)], axis=2)
                   for g in range(4)]).astype(f)
    wM = np.stack([np.concatenate([cols(o_ml + m * 128, 128), cols(o_ml + 1024 + m * 128, 128)], axis=2)
                   for m in range(8)]).astype(f)

    def rows(w):
        return np.ascontiguousarray(w[0].reshape(8, 128, 1024).transpose(1, 0, 2), dtype=f)
    wO3 = np.stack([rows(w_rnn_out), rows(w_attn_out), rows(w_out)])
    wG = np.zeros((2, 128, 8, 128), f)
    for i, wg in enumerate((w_rg_a[0], w_rg_x[0])):
        for c in range(8):
            wG[i, 0:64, c, 0:64] = wg[2 * c]
            wG[i, 64:128, c, 64:128] = wg[2 * c + 1]
    pvec = np.zeros((128, PV_N), f)
    pvec[:, PV_CW:PV_CW + 32] = conv_w[0].reshape(4, 8, 128).transpose(2, 1, 0).reshape(128, 32)
    pvec[:, PV_CB:PV_CB + 8] = conv_b[0].reshape(8, 128).T
    pvec[:, PV_BA:PV_BA + 8] = b_rg_a[0].reshape(8, 128).T
    pvec[:, PV_BX:PV_BX + 8] = b_rg_x[0].reshape(8, 128).T
    pvec[:, PV_LAM:PV_LAM + 8] = lru_lambda[0].reshape(8, 128).T
    sk = attn_sinks[0].reshape(8, 2)
    pvec[0:64, PV_SINK:PV_SINK + 8] = sk[:, 0][None, :]
    pvec[64:128, PV_SINK:PV_SINK + 8] = sk[:, 1][None, :]
    pvec[:, PV_BG:PV_BG + 16] = b_gate[0].reshape(16, 128).T
    gbc = np.stack([np.broadcast_to(pre_norm_g[0][None, :], (128, 1024)),
                    np.broadcast_to(post_norm_g[0][None, :], (128, 1024))]).astype(f)
    kk = np.arange(128)[:, None]
    qq = np.arange(128)[None, :]
    dm = np.empty((128, 256), f)
    dm[:, 0:128] = np.where(qq >= kk, -(qq - kk), -1.0e5)
    dm[:, 128:256] = np.where(qq < kk, -(128 + qq - kk), -1.0e5)
    return dict(wR=wR, wA=wA, wM=wM, wO3=wO3, wG=wG, pvec=pvec, gbc=np.ascontiguousarray(gbc), dmask=dm)


_NC_CACHE = {}


def kernel(x, pre_norm_g, w_in, b_gate, conv_w, conv_b, w_rg_a, b_rg_a, w_rg_x, b_rg_x,
           lru_lambda, attn_sinks, w_rnn_out, w_attn_out, w_out, post_norm_g):
    x = np.asarray(x, dtype=np.float32)
    shared = _layouts(*[np.asarray(a, dtype=np.float32) for a in (
        pre_norm_g, w_in, b_gate, conv_w, conv_b, w_rg_a, b_rg_a, w_rg_x, b_rg_x,
        lru_lambda, attn_sinks, w_rnn_out, w_attn_out, w_out, post_norm_g)])
    nc = build_program()
    in_maps = []
    for b in range(N_CORES):
        m = dict(shared)
        m["x"] = np.ascontiguousarray(x[b])
        in_maps.append(m)
    res = run_bass_kernel_spmd(nc, in_maps, core_ids=list(range(N_CORES)))
    return np.stack([np.asarray(r["out"], dtype=np.float32) for r in res.results], axis=0)
```

```python
import numpy as np
from contextlib import ExitStack
import concourse.bass as bass
import concourse.mybir as mybir
from concourse.bass_utils import run_bass_kernel_spmd

F32 = mybir.dt.float32
BF16 = mybir.dt.bfloat16
AF = mybir.ActivationFunctionType
ALU = mybir.AluOpType

D = 1024
T = 4096
TC = 512
NCH = T // TC
EPS = 1e-6
N_CORES = 8
ENGINES = ("pe", "act", "dve", "pool", "sp")


class Op:
    __slots__ = ("idx", "eng", "fn", "deps", "dma", "signal", "semid", "semval", "tag")

    def __init__(self, idx, eng, fn, dma):
        self.idx = idx
        self.eng = eng
        self.fn = fn
        self.dma = dma
        self.deps = set()
        self.signal = False
        self.semid = None
        self.semval = 0


class Sched:
    N_DMA_SEMS = 24
    count_pe = False

    def __init__(self, nc):
        self.nc = nc
        self.ops = []
        self.last_writer = {}
        self.readers = {}
        self.dma_count = 0
        self.dma_last_on_sem = {}
        self.pe_map = []

    stage = ""

    def op(self, eng, fn, reads=(), writes=(), dma=False, own_sem=False):
        o = Op(len(self.ops), eng, fn, dma)
        o.tag = self.stage
        for t in reads:
            w = self.last_writer.get(t)
            if w is not None:
                o.deps.add(w)
        for t in writes:
            w = self.last_writer.get(t)
            if w is not None:
                o.deps.add(w)
            for r in self.readers.get(t, ()):
                o.deps.add(r)
        for t in writes:
            self.last_writer[t] = o.idx
            self.readers[t] = []
        for t in reads:
            if t not in writes:
                self.readers.setdefault(t, []).append(o.idx)
        if dma and own_sem:
            self.n_own = getattr(self, "n_own", 0) + 1
            o.semid = ("own", self.n_own)
        elif dma:
            k = self.dma_count % self.N_DMA_SEMS
            self.dma_count += 1
            prev = self.dma_last_on_sem.get(k)
            if prev is not None:
                o.deps.add(prev)
            self.dma_last_on_sem[k] = o.idx
            o.semid = ("dma", k)
        else:
            o.semid = ("eng", eng)
        o.deps.discard(o.idx)
        self.ops.append(o)
        return o

    def emit(self):
        nc = self.nc
        ops = self.ops
        for o in ops:
            for d in o.deps:
                p = ops[d]
                if p.dma:
                    p.signal = True
                elif p.eng == "pe" and o.eng == "pe" and not o.dma:
                    continue
                else:
                    p.signal = True
        counters = {}
        for o in ops:
            if o.dma:
                o.signal = True
            if o.signal:
                inc = 16 if o.dma else 1
                counters[o.semid] = counters.get(o.semid, 0) + inc
                o.semval = counters[o.semid]
        semids = sorted(counters.keys(), key=str)
        with ExitStack() as es:
            sems = {}
            for sid in semids:
                sems[sid] = es.enter_context(nc.semaphore("s_%s_%s" % sid))
            block = es.enter_context(nc.Block())
            per_eng = {e: [o for o in ops if o.eng == e] for e in ENGINES}

            def run(e, handle):
                seen = {}
                cnt = [0]

                class _P:
                    def matmul(self_, *a, **k):
                        cnt[0] += 1
                        return handle.matmul(*a, **k)

                    def transpose(self_, *a, **k):
                        cnt[0] += 1
                        return handle.transpose(*a, **k)
                proxy = _P()
                for o in per_eng[e]:
                    need = {}
                    for d in o.deps:
                        p = ops[d]
                        if not p.signal:
                            continue
                        if (not p.dma) and p.eng == "pe" and e == "pe" and not o.dma:
                            continue
                        if p.semval > need.get(p.semid, 0):
                            need[p.semid] = p.semval
                    for sid, v in need.items():
                        if seen.get(sid, 0) >= v:
                            continue
                        seen[sid] = v
                        handle.wait_ge(sems[sid], v)
                    if e == "pe" and self.count_pe:
                        n0 = cnt[0]
                        ins = o.fn(proxy)
                        self.pe_map.append((o.tag, n0, cnt[0]))
                    else:
                        ins = o.fn(handle)
                    if o.signal:
                        ins.then_inc(sems[o.semid], 16 if o.dma else 1)

            if per_eng["pe"]:
                @block.tensor
                def _(h):
                    run("pe", h)
            if per_eng["act"]:
                @block.scalar
                def _(h):
                    run("act", h)
            if per_eng["dve"]:
                @block.vector
                def _(h):
                    run("dve", h)
            if per_eng["pool"]:
                @block.gpsimd
                def _(h):
                    run("pool", h)
            if per_eng["sp"]:
                @block.sync
                def _(h):
                    run("sp", h)


PV_CW = 0
PV_CB = 32
PV_BA = 40
PV_BX = 48
PV_LAM = 56
PV_SINK = 64
PV_BG = 72
PV_N = 88

SLOPES = [2.0 ** (-8.0 * (i + 1) / 16.0) for i in range(16)]
EW = "pool"


def build_program(nch=NCH, last_stages="FRAMO", ndma=24, debug=False):
    nc = bass.Bass("TRN2", target_bir_lowering=False)
    x = nc.dram_tensor("x", [T, D], F32, kind="ExternalInput").ap()
    wR = nc.dram_tensor("wR", [8, 128, 8, 256], F32, kind="ExternalInput").ap()
    wA = nc.dram_tensor("wA", [4, 128, 8, 704], F32, kind="ExternalInput").ap()
    wM = nc.dram_tensor("wM", [8, 128, 8, 256], F32, kind="ExternalInput").ap()
    wO3 = nc.dram_tensor("wO3", [3, 128, 8, 1024], F32, kind="ExternalInput").ap()
    wG = nc.dram_tensor("wG", [2, 128, 8, 128], F32, kind="ExternalInput").ap()
    pvec = nc.dram_tensor("pvec", [128, PV_N], F32, kind="ExternalInput").ap()
    gbc = nc.dram_tensor("gbc", [2, 128, 1024], F32, kind="ExternalInput").ap()
    dmask_d = nc.dram_tensor("dmask", [128, 256], F32, kind="ExternalInput").ap()
    out = nc.dram_tensor("out", [T, D], F32, kind="ExternalOutput").ap()

    if debug:
        dbg_mrg = nc.dram_tensor("dbg_mrg", [128, 8 * TC], BF16, kind="ExternalOutput").ap()
        dbg_ot = nc.dram_tensor("dbg_ot", [128, 1024], F32, kind="ExternalOutput").ap()
        dbg_sso = nc.dram_tensor("dbg_sso", [128, 16], F32, kind="ExternalOutput").ap()
        dbg_xt = nc.dram_tensor("dbg_xt", [128, 1024], F32, kind="ExternalOutput").ap()
        dbg_wo = nc.dram_tensor("dbg_wo", [128, 8 * 1024], BF16, kind="ExternalOutput").ap()
        dbg_hT = nc.dram_tensor("dbg_hT", [128, 8 * TC], BF16, kind="ExternalOutput").ap()
        dbg_ygr = nc.dram_tensor("dbg_ygr", [128, 8 * TC], BF16, kind="ExternalOutput").ap()
        dbg_yga = nc.dram_tensor("dbg_yga", [128, 8 * TC], BF16, kind="ExternalOutput").ap()
    S = Sched(nc)
    S.N_DMA_SEMS = ndma
    sb = nc.alloc_sbuf_tensor

    wro = sb("wro", [128, 8, 1024], BF16)
    wao = sb("wao", [128, 8, 1024], BF16)
    wo = sb("wo", [128, 8, 1024], BF16)
    wga = sb("wga", [128, 8, 128], BF16)
    wgx = sb("wgx", [128, 8, 128], BF16)
    gpre = sb("gpre", [128, 1024], F32)
    gpost = sb("gpost", [128, 1024], F32)
    dmask = sb("dmask_sb", [128, 256], F32)
    pv = sb("pv", [128, PV_N], F32)
    dc = sb("dc", [128, 64], F32)
    DC_HBA, DC_HBX, DC_S, DC_HS, DC_ESINK, DC_HBG = 0, 8, 16, 24, 32, 40
    ident = sb("ident", [128, 128], BF16)
    identf = sb("identf", [128, 128], F32)
    ones = sb("ones", [128, 64], BF16)
    wslot = [sb("wslot%d" % i, [128, 8 * 448], BF16) for i in range(2)]

    def wv(i, ncols):
        return wslot[i][:, 0:8 * ncols].rearrange("p (d c) -> p d c", d=8)
    xt = [sb("xt%d" % i, [128, 1024], F32) for i in range(2)]
    hTs = [sb("hT%d" % i, [128, 8, TC], BF16) for i in range(2)]
    ygrs = [sb("ygr%d" % i, [128, 8, TC], BF16) for i in range(2)]
    Mt = sb("Mt", [128, 4, TC], F32)
    CH = {"p": 0}
    yga = sb("yga", [128, 8, TC], BF16)
    mrg = sb("mrg", [128, 8, TC], BF16)
    qT = sb("qT", [128, 2, TC], BF16)
    kbuf = [sb("kbuf%d" % g, [128, 128 + TC], BF16) for g in range(4)]
    vbuf = [sb("vbuf%d" % g, [128, 5, 64], BF16) for g in range(4)]
    cstate = sb("cstate", [128, 8, 4], F32)
    hstate = sb("hstate", [128, 8], F32)
    ssf = sb("ssf", [128, 8], F32)
    rstf = sb("rstf", [128, 8], F32)
    sso = sb("sso", [128, 16], F32)
    NG = 26
    G = sb("G", [128, NG, 512], F32)
    xr = [sb("xr%d" % i, [128, 3 + TC], F32) for i in range(2)]

    def g32(i):
        return G[:, i, :]

    def gbf(i):
        return G[:, i, :].bitcast(BF16)

    HB0 = 13

    def hbv(tt):
        return gbf(HB0 + tt)

    PS2 = [nc.alloc_psum_tensor("ps2_%d" % i, [128, 1024], F32) for i in range(4)]

    def bank(b):
        return PS2[b // 2][:, (b % 2) * 512:(b % 2 + 1) * 512]

    def btok(b):
        return ("bank", b)

    dma_w = "pool"

    S.op("sp", lambda e: e.dma_start(out=pv[:], in_=pvec), writes=["pv"], dma=True)
    S.op("sp", lambda e: e.dma_start(out=gpre[:], in_=gbc[0]), writes=["gpre"], dma=True)
    S.op("sp", lambda e: e.dma_start(out=dmask[:], in_=dmask_d), writes=["dmask"], dma=True)
    scrR = nc.dram_tensor("scrR", [8, 128, 8, 256], BF16, kind="Internal").ap()
    scrAq = nc.dram_tensor("scrAq", [4, 128, 8, 448], BF16, kind="Internal").ap()
    scrAg = nc.dram_tensor("scrAg", [4, 128, 8, 256], BF16, kind="Internal").ap()
    scrM = nc.dram_tensor("scrM", [8, 128, 8, 256], BF16, kind="Internal").ap()
    S.op("dve", lambda e: e.memset(identf[:], 0.0), writes=["identf"])
    S.op("pool", lambda e: e.affine_select(out=identf[:], in_=identf[:], pattern=[[-1, 128]],
                                           compare_op=ALU.not_equal, fill=1.0, base=0,
                                           channel_multiplier=1),
         reads=["identf"], writes=["identf"])
    S.op("dve", lambda e: e.tensor_copy(out=ident[:], in_=identf[:]), reads=["identf"], writes=["ident"])
    S.op(dma_w, lambda e: e.dma_start(out=wv(0, 256), in_=wR[0]), writes=[("wslot", 0)], dma=True,
         own_sem=True)
    S.op(dma_w, lambda e: e.dma_start(out=wv(1, 256), in_=wR[1]), writes=[("wslot", 1)], dma=True,
         own_sem=True)
    S.op(dma_w, lambda e: e.dma_start(out=scrR[0:2], in_=wR[0:2]), writes=["scrR0"], dma=True, own_sem=True)
    S.op(dma_w, lambda e: e.dma_start(out=wga[:], in_=wG[0]), writes=["wga"], dma=True, own_sem=True)
    S.op(dma_w, lambda e: e.dma_start(out=wgx[:], in_=wG[1]), writes=["wgx"], dma=True, own_sem=True)

    def late_casts_1():
        S.op(dma_w, lambda e: e.dma_start(out=scrR[2:8], in_=wR[2:8]), writes=["scrR1"], dma=True, own_sem=True)
        S.op(dma_w, lambda e: e.dma_start(out=scrAq, in_=wA[:, :, :, 0:448]), writes=["scrAq"], dma=True, own_sem=True)
        S.op(dma_w, lambda e: e.dma_start(out=scrAg, in_=wA[:, :, :, 448:704]), writes=["scrAg"], dma=True, own_sem=True)

    def late_casts_2():
        S.op(dma_w, lambda e: e.dma_start(out=wro[:], in_=wO3[0]), writes=["wro"], dma=True, own_sem=True)
        S.op(dma_w, lambda e: e.dma_start(out=scrM, in_=wM), writes=["scrM"], dma=True, own_sem=True)
        S.op(dma_w, lambda e: e.dma_start(out=wao[:], in_=wO3[1]), writes=["wao"], dma=True, own_sem=True)
        S.op(dma_w, lambda e: e.dma_start(out=wo[:], in_=wO3[2]), writes=["wo"], dma=True, own_sem=True)
        S.op("sp", lambda e: e.dma_start(out=gpost[:], in_=gbc[1]), writes=["gpost"], dma=True)
    S.op("dve", lambda e: e.memset(ones[:], 1.0), writes=["ones"])
    nhalf = sb("nhalf", [128, 8], F32)
    S.op("dve", lambda e: e.memset(nhalf[:], -0.5), writes=["nhalf"])
    negm = sb("negm", [128, 128], F32)
    S.op("dve", lambda e: e.memset(negm[:], -1.0e5), writes=["negm"])
    for g_ in range(4):
        S.op("dve", (lambda g_=g_: lambda e: e.memset(kbuf[g_][:, 0:128], 0.0))(), writes=[("kprev", g_)])
        S.op("dve", (lambda g_=g_: lambda e: e.memset(vbuf[g_][:, 0, :], 0.0))(), writes=[("vprev", g_)])
    S.op("dve", lambda e: e.memset(cstate[:], 0.0), writes=["cstate"])
    S.op("dve", lambda e: e.memset(hstate[:], 0.0), writes=["hstate"])
    S.op("dve", lambda e: e.tensor_scalar(out=dc[:, DC_HBA:DC_HBA + 16], in0=pv[:, PV_BA:PV_BA + 16],
                                          scalar1=0.5, scalar2=None, op0=ALU.mult),
         reads=["pv"], writes=["dc_hb"])
    S.op("dve", lambda e: e.tensor_scalar(out=dc[:, DC_HBG:DC_HBG + 16], in0=pv[:, PV_BG:PV_BG + 16],
                                          scalar1=0.5, scalar2=None, op0=ALU.mult),
         reads=["pv"], writes=["dc_hbg"])
    S.op("act", lambda e: e.activation(out=dc[:, DC_S:DC_S + 8], in_=pv[:, PV_LAM:PV_LAM + 8],
                                       func=AF.Exp, scale=-1.0),
         reads=["pv"], writes=["dc_s"])
    S.op("act", lambda e: e.activation(out=dc[:, DC_S:DC_S + 8], in_=dc[:, DC_S:DC_S + 8],
                                       func=AF.Ln, bias=1.0, scale=1.0),
         reads=["dc_s"], writes=["dc_s"])
    S.op("dve", lambda e: e.tensor_scalar(out=dc[:, DC_HS:DC_HS + 8], in0=dc[:, DC_S:DC_S + 8],
                                          scalar1=-4.0, scalar2=None, op0=ALU.mult),
         reads=["dc_s"], writes=["dc_hs"])
    S.op("dve", lambda e: e.tensor_scalar(out=dc[:, DC_S:DC_S + 8], in0=dc[:, DC_S:DC_S + 8],
                                          scalar1=-8.0, scalar2=None, op0=ALU.mult),
         reads=["dc_s", "dc_hs"], writes=["dc_s"])
    S.op("act", lambda e: e.activation(out=dc[:, DC_ESINK:DC_ESINK + 8], in_=pv[:, PV_SINK:PV_SINK + 8],
                                       func=AF.Exp),
         reads=["pv"], writes=["dc_esink"])
    S.op("dve", lambda e: e.tensor_scalar(out=dc[:, 56:64], in0=dc[:, DC_ESINK:DC_ESINK + 8],
                                          scalar1=2.0, scalar2=None, op0=ALU.mult),
         reads=["dc_esink"], writes=["dc_esink2"])

    piece_seq = []
    _units = [(kind, m) for m in range(8) for kind in "LRA"]
    _cost = {"L": 4.2, "R": 2.1, "A": 2.1}
    _rk = {k: (4.2 if k == -2 else 4.7 if k <= 5 else 0.5 if k == 6 else 0.0) for k in range(-2, 9)}
    _budget = (sum(_rk.values()) + sum(_cost[u[0]] for u in _units)) / 11.0
    M_SCHED = {}
    _ui = 0
    for _k in range(-2, 9):
        load = _rk[_k]
        M_SCHED[_k] = []
        while _ui < len(_units) and (_k == 8 or load + 0.5 * _cost[_units[_ui][0]] <= _budget):
            M_SCHED[_k].append(_units[_ui])
            load += _cost[_units[_ui][0]]
            _ui += 1

    def _rp(c):
        return (scrR[c], 256, "scrR0" if c < 2 else "scrR1")
    for _ch in range(nch):
        for _k in range(-2, 9):
            if 0 <= _k + 2 < 8:
                piece_seq.append(_rp(_k + 2))
            if _ch > 0:
                for _kind, _m in M_SCHED[_k]:
                    if _kind == "L":
                        piece_seq.append((scrM[_m], 256, "scrM"))
        for _g in range(4):
            piece_seq.append((scrAq[_g], 448, "scrAq"))
            piece_seq.append((scrAg[_g], 256, "scrAg"))
    piece_seq += [(scrM[m], 256, "scrM") for m in range(8)]
    piece_state = {"issued": 2, "used": 0}

    def issue_piece():
        n = piece_state["issued"]
        if n >= len(piece_seq):
            return
        piece_state["issued"] += 1
        src_ap, ncols, stok = piece_seq[n]
        i = n % 2
        S.op("sp", lambda e: e.dma_start(out=wv(i, ncols), in_=src_ap),
             reads=[stok], writes=[("wslot", i)], dma=True)

    def load_piece(src_ap, ncols):
        n = piece_state["used"]
        piece_state["used"] += 1
        while piece_state["issued"] <= n:
            issue_piece()
        if piece_state["issued"] <= n + 1:
            issue_piece()
        return n % 2

    big_loaded = [False]

    def load_big():
        if big_loaded[0]:
            return
        big_loaded[0] = True
        pass

    JUNK = 12

    def F_pre(ch, parts=(0, 1, 2)):
        T0 = ch * TC
        rp = (ch % 2) * 4

        def ld(tt):
            r0 = T0 + tt * 128
            S.op("sp", (lambda tt=tt, r0=r0: lambda e: e.dma_start(out=xt[tt % 2][:], in_=x[r0:r0 + 128, :]))(),
                 writes=[("xt", tt % 2)], dma=True)

        def proc(tt):
            sl = tt % 2
            col = rp + tt
            S.op("act", (lambda sl=sl, col=col: lambda e: e.activation(
                out=gbf(JUNK), in_=xt[sl][:], func=AF.Square, accum_out=ssf[:, col:col + 1]))(),
                 reads=[("xt", sl)], writes=[("G", JUNK), ("ssf", col)])
            S.op("pool", (lambda col=col: lambda e: e.tensor_scalar(
                out=rstf[:, col:col + 1], in0=ssf[:, col:col + 1], scalar1=1.0 / D, scalar2=EPS,
                op0=ALU.mult, op1=ALU.add))(),
                 reads=[("ssf", col)], writes=[("rstf", col)])
            S.op("pool", (lambda col=col: lambda e: e.tensor_tensor(
                out=rstf[:, col:col + 1], in0=rstf[:, col:col + 1], in1=nhalf[:, 0:1], op=ALU.pow))(),
                 reads=[("rstf", col), "nhalf"], writes=[("rstf", col)])
            S.op("dve", (lambda tt=tt, sl=sl, col=col: lambda e: e.scalar_tensor_tensor(
                out=hbv(tt), in0=xt[sl][:], scalar=rstf[:, col:col + 1], in1=gpre[:],
                op0=ALU.mult, op1=ALU.mult))(),
                 reads=[("xt", sl), ("rstf", col), "gpre"], writes=[("G", HB0 + tt)])
        if 0 in parts:
            ld(0)
            ld(1)
        if 1 in parts:
            proc(0)
            proc(1)
            ld(2)
            ld(3)
        if 2 in parts:
            proc(2)
            proc(3)

    def F_tr(ch):
        for tt in range(4):
            pb = tt % 2

            def tr(e, tt=tt, pb=pb):
                pbv = bank(pb).bitcast(BF16)
                ins = None
                for d in range(8):
                    ins = e.transpose(pbv[:, d * 128:(d + 1) * 128], hbv(tt)[:, d * 128:(d + 1) * 128], ident[:])
                return ins
            S.op("pe", tr, reads=[("G", HB0 + tt), "ident"], writes=[btok(pb)])
            S.op("act", (lambda tt=tt, pb=pb, hTc=hTs[ch % 2]: lambda e: e.activation(
                out=hTc[:, :, tt * 128:(tt + 1) * 128],
                in_=bank(pb).bitcast(BF16).rearrange("p (d t) -> p d t", d=8),
                func=AF.Identity))(),
                 reads=[btok(pb)], writes=[("hT", ch % 2)])

    def R_tiles(c):
        par = c % 2
        gb = par * 13
        return par, gb, par * 4

    def RB_G(c):
        return 2 + (c % 2)

    def RB_X(c):
        return c % 2
    RB_R, RB_I = 4, 5
    MB = [6, 7]

    def R0(c):
        ws = load_piece(wR[c], 256)

        p = CH["p"]

        def mm(col0, b, ws=ws, hTc=hTs[p]):
            def f(e):
                ins = None
                for d in range(8):
                    ins = e.matmul(bank(b), lhsT=wv(ws, 256)[:, d, col0:col0 + 128], rhs=hTc[:, d, :],
                                   start=(d == 0), stop=(d == 7))
                return ins
            return f
        S.op("pe", mm(0, RB_X(c)), reads=[("wslot", ws), ("hT", p)], writes=[btok(RB_X(c))])
        S.op("pe", mm(128, RB_G(c)), reads=[("wslot", ws), ("hT", p)], writes=[btok(RB_G(c))])

    def R1(c):
        par, gb, _pb = R_tiles(c)
        pbase = RB_X(c)
        xr_ = xr[par]
        cv = gb + 0

        S.op(EW, (lambda c=c, xr_=xr_: lambda e: e.tensor_copy(out=xr_[:, 0:3], in_=cstate[:, c, 0:3]))(),
             reads=["cstate"], writes=[("xrh", par)])
        S.op("act", (lambda xr_=xr_, pbase=pbase: lambda e: e.activation(
            out=xr_[:, 3:3 + TC], in_=bank(pbase), func=AF.Identity))(),
             reads=[btok(pbase)], writes=[("xr", par)])
        S.op(EW, (lambda c=c, xr_=xr_: lambda e: e.tensor_copy(out=cstate[:, c, 0:3], in_=xr_[:, TC:TC + 3]))(),
             reads=[("xr", par)], writes=["cstate"])
        S.op("dve", (lambda c=c, xr_=xr_, cv=cv: lambda e: e.tensor_scalar(
            out=g32(cv), in0=xr_[:, 3:3 + TC], scalar1=pv[:, PV_CW + c * 4 + 3:PV_CW + c * 4 + 4],
            scalar2=pv[:, PV_CB + c:PV_CB + c + 1], op0=ALU.mult, op1=ALU.add))(),
             reads=[("xr", par), "pv"], writes=[("G", cv)])
    def R1b(c):
        par, gb, pbase = R_tiles(c)
        xr_ = xr[par]
        cv = gb + 0
        for k in (2, 1, 0):
            S.op("dve", (lambda c=c, xr_=xr_, cv=cv, k=k: lambda e: e.scalar_tensor_tensor(
                out=g32(cv), in0=xr_[:, k:k + TC], scalar=pv[:, PV_CW + c * 4 + k:PV_CW + c * 4 + k + 1],
                in1=g32(cv), op0=ALU.mult, op1=ALU.add))(),
                 reads=[("xr", par), ("xrh", par), "pv", ("G", cv)], writes=[("G", cv)])

    def R2(c):
        par, gb, _pb = R_tiles(c)
        pb1, pb2, pb3 = RB_G(c), RB_R, RB_I
        cv, tr_, ti_, a_, e2_, hm_, w_, u_, y_, tg_, tmp2_, cvb_i = [gb + k for k in range(12)]
        S.op("act", (lambda cv=cv, cvb_i=cvb_i: lambda e: e.activation(
            out=gbf(cvb_i)[:, 0:TC], in_=g32(cv), func=AF.Identity))(),
             reads=[("G", cv)], writes=[("G", cvb_i)])
        S.op("pe", (lambda c=c, cvb_i=cvb_i, pb1=pb1, pb2=pb2, pb3=pb3: lambda e: e.matmul(
            bank(pb2), lhsT=wga[:, c, :], rhs=gbf(cvb_i)[:, 0:TC], start=True, stop=True))(),
             reads=["wga", ("G", cvb_i)], writes=[btok(pb2)])
        S.op("pe", (lambda c=c, cvb_i=cvb_i, pb1=pb1, pb2=pb2, pb3=pb3: lambda e: e.matmul(
            bank(pb3), lhsT=wgx[:, c, :], rhs=gbf(cvb_i)[:, 0:TC], start=True, stop=True))(),
             reads=["wgx", ("G", cvb_i)], writes=[btok(pb3)])
        S.op("act", (lambda c=c, tr_=tr_, pb1=pb1, pb2=pb2, pb3=pb3: lambda e: e.activation(
            out=g32(tr_), in_=bank(pb2), func=AF.Tanh, scale=0.5,
            bias=dc[:, DC_HBA + c:DC_HBA + c + 1]))(),
             reads=[btok(pb2), "dc_hb"], writes=[("G", tr_)])
        S.op("act", (lambda c=c, ti_=ti_, pb1=pb1, pb2=pb2, pb3=pb3: lambda e: e.activation(
            out=g32(ti_), in_=bank(pb3), func=AF.Tanh, scale=0.5,
            bias=dc[:, DC_HBX + c:DC_HBX + c + 1]))(),
             reads=[btok(pb3), "dc_hb"], writes=[("G", ti_)])
        S.op("act", (lambda c=c, tr_=tr_, a_=a_: lambda e: e.activation(
            out=g32(a_), in_=g32(tr_), func=AF.Exp, scale=dc[:, DC_HS + c:DC_HS + c + 1],
            bias=dc[:, DC_HS + c:DC_HS + c + 1]))(),
             reads=[("G", tr_), "dc_hs"], writes=[("G", a_)])
        S.op("act", (lambda c=c, tr_=tr_, e2_=e2_: lambda e: e.activation(
            out=g32(e2_), in_=g32(tr_), func=AF.Exp, scale=dc[:, DC_S + c:DC_S + c + 1],
            bias=dc[:, DC_S + c:DC_S + c + 1]))(),
             reads=[("G", tr_), "dc_s"], writes=[("G", e2_)])
        S.op("act", (lambda tg_=tg_, pb1=pb1, pb2=pb2, pb3=pb3: lambda e: e.activation(
            out=g32(tg_), in_=bank(pb1), func=AF.Tanh, scale=0.5))(),
             reads=[btok(pb1)], writes=[("G", tg_)])
        S.op("dve", (lambda ti_=ti_, cv=cv, w_=w_: lambda e: e.scalar_tensor_tensor(
            out=g32(w_), in0=g32(ti_), scalar=1.0, in1=g32(cv), op0=ALU.add, op1=ALU.mult))(),
             reads=[("G", ti_), ("G", cv)], writes=[("G", w_)])
        S.op("dve", (lambda tg_=tg_, tmp2_=tmp2_, pb1=pb1, pb2=pb2, pb3=pb3: lambda e: e.scalar_tensor_tensor(
            out=g32(tmp2_), in0=g32(tg_), scalar=1.0, in1=bank(pb1), op0=ALU.add, op1=ALU.mult))(),
             reads=[("G", tg_), btok(pb1)], writes=[("G", tmp2_)])

    def R3a(c):
        par, gb, pbase = R_tiles(c)
        e2_, hm_ = gb + 4, gb + 5
        S.op("act", (lambda e2_=e2_, hm_=hm_: lambda e: e.activation(
            out=g32(hm_), in_=g32(e2_), func=AF.Sqrt, scale=-0.25, bias=0.25))(),
             reads=[("G", e2_)], writes=[("G", hm_)])

    def R3b(c):
        par, gb, pbase = R_tiles(c)
        cv, tr_, ti_, a_, e2_, hm_, w_, u_, y_, tg_, tmp2_, cvb_i = [gb + k for k in range(12)]
        S.op("dve", (lambda hm_=hm_, w_=w_, u_=u_: lambda e: e.tensor_tensor(
            out=g32(u_), in0=g32(hm_), in1=g32(w_), op=ALU.mult))(),
             reads=[("G", hm_), ("G", w_)], writes=[("G", u_)])
        S.op("dve", (lambda c=c, a_=a_, u_=u_, y_=y_: lambda e: e.tensor_tensor_scan(
            out=g32(y_), data0=g32(a_), data1=g32(u_), initial=hstate[:, c:c + 1],
            op0=ALU.mult, op1=ALU.add))(),
             reads=[("G", a_), ("G", u_), "hstate"], writes=[("G", y_)])
        S.op(EW, (lambda c=c, y_=y_: lambda e: e.tensor_copy(out=hstate[:, c:c + 1], in_=g32(y_)[:, TC - 1:TC]))(),
             reads=[("G", y_)], writes=["hstate"])
        S.op("dve", (lambda c=c, tmp2_=tmp2_, y_=y_, yg=ygrs[CH["p"]]: lambda e: e.scalar_tensor_tensor(
            out=yg[:, c, :], in0=g32(tmp2_), scalar=0.5, in1=g32(y_), op0=ALU.mult, op1=ALU.mult))(),
             reads=[("G", tmp2_), ("G", y_)], writes=[("ygr", CH["p"], c)])

    A_SC = [0, 1]
    A_SG = [[2, 3], [4, 5]]
    A_TGA, A_DS, A_YA = 6, 17, 25
    PT0 = 7
    a_rot = [0]
    a_ws = {}

    def A_P(g):
        p = CH["p"]
        hTc = hTs[p]
        ws = load_piece(None, 448)
        kb_, vb_ = kbuf[g], vbuf[g]

        def nextbank():
            b = a_rot[0] % 2
            a_rot[0] += 1
            return b

        def proj(col0, b, ws, nc_=448):
            def f(e, ws=ws, hTc=hTc, nc_=nc_):
                ins = None
                for d in range(8):
                    ins = e.matmul(bank(b), lhsT=wv(ws, nc_)[:, d, col0:col0 + 128], rhs=hTc[:, d, :],
                                   start=(d == 0), stop=(d == 7))
                return ins
            return f
        for j in range(2):
            b = nextbank()
            S.op("pe", proj(j * 128, b, ws), reads=[("wslot", ws), ("hT", p)], writes=[btok(b)])
            S.op("act", (lambda j=j, b=b: lambda e: e.activation(
                out=qT[:, j, :], in_=bank(b), func=AF.Identity, scale=0.125))(),
                 reads=[btok(b)], writes=[("qT", j)])
        b = nextbank()
        S.op("pe", proj(256, b, ws), reads=[("wslot", ws), ("hT", p)], writes=[btok(b)])
        S.op("act", (lambda kb_=kb_, b=b: lambda e: e.activation(
            out=kb_[:, 128:128 + TC], in_=bank(b), func=AF.Identity))(),
             reads=[btok(b)], writes=[("kcur", g)])
        b = nextbank()

        def mm_v(e, ws=ws, b=b, hTc=hTc):
            ins = None
            for tt in range(4):
                for d in range(8):
                    ins = e.matmul(bank(b)[:, tt * 64:(tt + 1) * 64], lhsT=hTc[:, d, tt * 128:(tt + 1) * 128],
                                   rhs=wv(ws, 448)[:, d, 384:448], start=(d == 0), stop=(d == 7))
            return ins
        S.op("pe", mm_v, reads=[("wslot", ws), ("hT", p)], writes=[btok(b)])
        S.op("act", (lambda vb_=vb_, b=b: lambda e: e.activation(
            out=vb_[:, 1:5, :], in_=bank(b)[:, 0:256].rearrange("p (t d) -> p t d", t=4),
            func=AF.Identity))(),
             reads=[btok(b)], writes=[("vcur", g)])
        ws2 = load_piece(None, 256)
        for j in range(2):
            b = nextbank()
            sg = A_SG[g % 2][j]
            S.op("pe", proj(j * 128, b, ws2, 256), reads=[("wslot", ws2), ("hT", p)], writes=[btok(b)])
            S.op("act", (lambda b=b: lambda e: e.activation(
                out=g32(A_TGA), in_=bank(b), func=AF.Tanh, scale=0.5))(),
                 reads=[btok(b)], writes=[("G", A_TGA)])
            S.op("dve", (lambda sg=sg, b=b: lambda e: e.scalar_tensor_tensor(
                out=g32(sg), in0=g32(A_TGA), scalar=1.0, in1=bank(b), op0=ALU.add, op1=ALU.mult))(),
                 reads=[("G", A_TGA), btok(b)], writes=[("G", sg)])

    def A_S(g, first):
        kb_ = kbuf[g]
        srot = 0
        for kb in range(5):
            if kb == 0:
                q0, w_, msk = 0, 128, (negm[:, 0:128] if first else dmask[:, 128:256])
            elif kb == 4:
                q0, w_, msk = 384, 128, dmask[:, 0:128]
            else:
                q0, w_, msk = (kb - 1) * 128, 256, dmask[:, 0:256]
            for ph in range(2):
                sbk = 2 + (srot % 2)
                sci = A_SC[srot % 2]
                srot += 1
                pr = slice(ph * 64, (ph + 1) * 64)
                S.op("pe", (lambda kb=kb, pr=pr, q0=q0, w_=w_, sbk=sbk, kb_=kb_: lambda e: e.matmul(
                    bank(sbk)[:, 0:2 * w_], lhsT=kb_[pr, kb * 128:(kb + 1) * 128], rhs=qT[pr, :, q0:q0 + w_],
                    start=True, stop=True))(),
                     reads=[("kcur", g), ("kprev", g), ("qT", 0), ("qT", 1)], writes=[btok(sbk)])
                for j in range(2):
                    head = 4 * g + 2 * j + ph
                    S.op("dve", (lambda j=j, head=head, w_=w_, sbk=sbk, sci=sci, msk=msk: lambda e: e.scalar_tensor_tensor(
                        out=g32(sci)[:, j * w_:(j + 1) * w_], in0=msk, scalar=float(SLOPES[head]),
                        in1=bank(sbk)[:, j * w_:(j + 1) * w_], op0=ALU.mult, op1=ALU.add))(),
                         reads=["dmask", "negm", btok(sbk)], writes=[("G", sci)])
                S.op("act", (lambda kb=kb, ph=ph, w_=w_, sci=sci: lambda e: e.activation(
                    out=gbf(PT0 + kb)[:, ph * 512:ph * 512 + 2 * w_], in_=g32(sci)[:, 0:2 * w_], func=AF.Exp))(),
                     reads=[("G", sci)], writes=[("PT", kb, ph), ("G", PT0 + kb)])

    def A_V(g):
        vb_ = vbuf[g]
        kb_ = kbuf[g]
        Ot, Dt = PS2[2], PS2[3]

        def ptview(kb, ph, part):
            base = gbf(PT0 + kb)[:, ph * 512:(ph + 1) * 512]
            if kb == 0 or kb == 4:
                return base[:, 0:256]
            return base.rearrange("p (j q) -> p j q", j=2)[:, :, part * 128:(part + 1) * 128]

        def mm_pv(e):
            ins = None
            for i in range(4):
                for ph in range(2):
                    pr = slice(ph * 64, (ph + 1) * 64)
                    terms = [(i, 1), (i + 1, 0)]
                    for ti, (kb, part) in enumerate(terms):
                        e.matmul(Ot[pr, i * 256:(i + 1) * 256], lhsT=vb_[:, kb, :], rhs=ptview(kb, ph, part),
                                 start=(ti == 0), stop=(ti == 1))
                    for ti, (kb, part) in enumerate(terms):
                        ins = e.matmul(Dt[pr, i * 256:(i + 1) * 256], lhsT=ones[:, 0:64], rhs=ptview(kb, ph, part),
                                       start=(ti == 0), stop=(ti == 1))
            return ins
        S.op("pe", mm_pv,
             reads=[("vcur", g), ("vprev", g), "ones"] + [("PT", kb, ph) for kb in range(5) for ph in range(2)]
             + [("G", PT0 + kb) for kb in range(5)],
             writes=[btok(4), btok(5), btok(6), btok(7)])
    def A_N(g):
        vb_ = vbuf[g]
        kb_ = kbuf[g]
        Ot, Dt = PS2[2], PS2[3]
        Ov = Ot[:].rearrange("p (i j q) -> p i j q", i=4, j=2)
        Dv = Dt[:].rearrange("p (i j q) -> p i j q", i=4, j=2)
        DSs = [17, 18]
        YAs = [25, 19]
        for j in range(2):
            cc = 2 * g + j
            S.op("act", (lambda j=j, cc=cc: lambda e: e.activation(
                out=g32(DSs[j]).rearrange("p (i q) -> p i q", i=4), in_=Dv[:, :, j, :], func=AF.Ln,
                scale=2.0, bias=dc[:, 56 + cc:56 + cc + 1]))(),
                 reads=[btok(6), btok(7), "dc_esink2"], writes=[("G", DSs[j])])
        for j in range(2):
            S.op("act", (lambda j=j: lambda e: e.activation(
                out=g32(DSs[j]), in_=g32(DSs[j]), func=AF.Exp, scale=-1.0))(),
                 reads=[("G", DSs[j])], writes=[("G", DSs[j])])
        for j in range(2):
            S.op("dve", (lambda j=j: lambda e: e.tensor_tensor(
                out=g32(YAs[j]).rearrange("p (i q) -> p i q", i=4), in0=Ov[:, :, j, :],
                in1=g32(DSs[j]).rearrange("p (i q) -> p i q", i=4), op=ALU.mult))(),
                 reads=[btok(4), btok(5), ("G", DSs[j])], writes=[("G", YAs[j])])
        for j in range(2):
            cc = 2 * g + j
            sg = A_SG[g % 2][j]
            S.op("dve", (lambda j=j, cc=cc, sg=sg: lambda e: e.tensor_tensor(
                out=yga[:, cc, :], in0=g32(YAs[j]), in1=g32(sg), op=ALU.mult))(),
                 reads=[("G", YAs[j]), ("G", sg)], writes=[("yga", cc)])
        S.op(EW, (lambda kb_=kb_: lambda e: e.tensor_copy(out=kb_[:, 0:128], in_=kb_[:, TC:TC + 128]))(),
             reads=[("kcur", g)], writes=[("kprev", g)])
        S.op(EW, (lambda vb_=vb_: lambda e: e.tensor_copy(out=vb_[:, 0, :], in_=vb_[:, 4, :]))(),
             reads=[("vcur", g)], writes=[("vprev", g)])

    m_ws = {}

    def M_a(m, p, part="LR"):
        par = m % 2
        ws = load_piece(None, 256) if "L" in part else None
        hTc, ygc = hTs[p], ygrs[p]
        t_r, t_a = Mt[:, par * 2 + 0, :], Mt[:, par * 2 + 1, :]
        tk_r, tk_a = ("Mt", par, 0), ("Mt", par, 1)

        def mm_l(col0, b, ws=ws, hTc=hTc):
            def f(e):
                ins = None
                for d in range(8):
                    ins = e.matmul(bank(b), lhsT=wv(ws, 256)[:, d, col0:col0 + 128], rhs=hTc[:, d, :],
                                   start=(d == 0), stop=(d == 7))
                return ins
            return f

        def mm_b(wt, src, b, m=m):
            def f(e):
                ins = None
                for c in range(8):
                    ins = e.matmul(bank(b), lhsT=wt[:, c, m * 128:(m + 1) * 128], rhs=src[:, c, :],
                                   start=(c == 0), stop=(c == 7))
                return ins
            return f
        if "L" in part:
            S.op("pe", mm_l(0, MB[0]), reads=[("wslot", ws), ("hT", p)], writes=[btok(MB[0])])
            S.op("pe", mm_l(128, MB[1]), reads=[("wslot", ws), ("hT", p)], writes=[btok(MB[1])])
            S.op("act", (lambda m=m, t_r=t_r: lambda e: e.activation(
                out=t_r, in_=bank(MB[0]), func=AF.Tanh, scale=0.5, bias=dc[:, DC_HBG + m:DC_HBG + m + 1]))(),
                 reads=[btok(MB[0]), "dc_hbg"], writes=[tk_r])
            S.op("act", (lambda m=m, t_a=t_a: lambda e: e.activation(
                out=t_a, in_=bank(MB[1]), func=AF.Tanh, scale=0.5,
                bias=dc[:, DC_HBG + 8 + m:DC_HBG + 8 + m + 1]))(),
                 reads=[btok(MB[1]), "dc_hbg"], writes=[tk_a])
        if "R" not in part:
            return
        S.op("pe", mm_b(wro, ygc, MB[0]), reads=["wro"] + [("ygr", p, c) for c in range(8)],
             writes=[btok(MB[0])])
        S.op("dve", (lambda t_r=t_r: lambda e: e.scalar_tensor_tensor(
            out=t_r, in0=t_r, scalar=1.0, in1=bank(MB[0]), op0=ALU.add, op1=ALU.mult))(),
             reads=[tk_r, btok(MB[0])], writes=[tk_r])

    def M_b(m, p):
        par = m % 2
        t_r, t_a = Mt[:, par * 2 + 0, :], Mt[:, par * 2 + 1, :]
        tk_r, tk_a = ("Mt", par, 0), ("Mt", par, 1)

        def mm_b(wt, src, b, m=m):
            def f(e):
                ins = None
                for c in range(8):
                    ins = e.matmul(bank(b), lhsT=wt[:, c, m * 128:(m + 1) * 128], rhs=src[:, c, :],
                                   start=(c == 0), stop=(c == 7))
                return ins
            return f
        S.op("pe", mm_b(wao, yga, MB[1]), reads=["wao"] + [("yga", c) for c in range(8)],
             writes=[btok(MB[1])])
        S.op("dve", (lambda t_a=t_a: lambda e: e.scalar_tensor_tensor(
            out=t_a, in0=t_a, scalar=1.0, in1=bank(MB[1]), op0=ALU.add, op1=ALU.mult))(),
             reads=[tk_a, btok(MB[1])], writes=[tk_a])
        S.op("dve", (lambda m=m, t_r=t_r, t_a=t_a: lambda e: e.tensor_tensor(
            out=mrg[:, m, :], in0=t_r, in1=t_a, op=ALU.add))(),
             reads=[tk_r, tk_a], writes=[("mrg", m)])

    OT0 = 14

    def O_a(ch, tt):
        pbase = 4 + (tt % 2) * 2
        for hf in range(2):
            def mm_o(e, tt=tt, hf=hf, pbase=pbase):
                ins = None
                for m in range(8):
                    ins = e.matmul(bank(pbase + hf), lhsT=mrg[:, m, tt * 128:(tt + 1) * 128],
                                   rhs=wo[:, m, hf * 512:(hf + 1) * 512], start=(m == 0), stop=(m == 7))
                return ins
            S.op("pe", mm_o, reads=["wo"] + [("mrg", m) for m in range(8)], writes=[btok(pbase + hf)])
            S.op("act", (lambda tt=tt, hf=hf, pbase=pbase: lambda e: e.activation(
                out=gbf(JUNK)[:, 0:512], in_=bank(pbase + hf), func=AF.Square,
                accum_out=sso[:, tt * 2 + hf:tt * 2 + hf + 1]))(),
                 reads=[btok(pbase + hf)], writes=[("G", JUNK), ("sso", tt, hf)])
        S.op("dve", (lambda tt=tt: lambda e: e.tensor_tensor(
            out=sso[:, 8 + tt:9 + tt], in0=sso[:, tt * 2:tt * 2 + 1], in1=sso[:, tt * 2 + 1:tt * 2 + 2],
            op=ALU.add))(),
             reads=[("sso", tt, 0), ("sso", tt, 1)], writes=[("ssr", tt)])
        S.op("pool", (lambda tt=tt: lambda e: e.tensor_scalar(
            out=sso[:, 8 + tt:9 + tt], in0=sso[:, 8 + tt:9 + tt], scalar1=1.0 / D, scalar2=4.0 * EPS,
            op0=ALU.mult, op1=ALU.add))(),
             reads=[("ssr", tt)], writes=[("ssr", tt)])
        S.op("pool", (lambda tt=tt: lambda e: e.tensor_tensor(
            out=sso[:, 8 + tt:9 + tt], in0=sso[:, 8 + tt:9 + tt], in1=nhalf[:, 0:1], op=ALU.pow))(),
             reads=[("ssr", tt), "nhalf"], writes=[("ssr", tt)])

    def O_b1(ch, tt):
        pbase = 4 + (tt % 2) * 2
        ot = G[:, OT0 + 2 * tt:OT0 + 2 * tt + 2, :]
        ottok = [("G", OT0 + 2 * tt), ("G", OT0 + 2 * tt + 1)]
        for hf in range(2):
            S.op("dve", (lambda tt=tt, hf=hf, pbase=pbase, ot=ot: lambda e: e.scalar_tensor_tensor(
                out=ot[:, hf, :], in0=bank(pbase + hf), scalar=sso[:, 8 + tt:9 + tt],
                in1=gpost[:, hf * 512:(hf + 1) * 512], op0=ALU.mult, op1=ALU.mult))(),
                 reads=[btok(pbase + hf), ("ssr", tt), "gpost"], writes=[ottok[hf]])

    def O_ld(ch, tt):
        r0 = ch * TC + tt * 128
        par = tt % 2
        xr_ap = G[:, 22 + 2 * par:24 + 2 * par, :].rearrange("p a b -> p (a b)")
        S.op("sp", (lambda r0=r0, xr_ap=xr_ap: lambda e: e.dma_start(out=xr_ap, in_=x[r0:r0 + 128, :]))(),
             writes=[("G", 22 + 2 * par), ("G", 23 + 2 * par)], dma=True)

    def O_b2(ch, tt):
        r0 = ch * TC + tt * 128
        par = tt % 2
        otf = G[:, OT0 + 2 * tt:OT0 + 2 * tt + 2, :].rearrange("p a b -> p (a b)")
        ottok = [("G", OT0 + 2 * tt), ("G", OT0 + 2 * tt + 1)]
        xr_ap = G[:, 22 + 2 * par:24 + 2 * par, :].rearrange("p a b -> p (a b)")
        S.op("dve", (lambda otf=otf, xr_ap=xr_ap: lambda e: e.tensor_tensor(out=otf, in0=otf, in1=xr_ap, op=ALU.add))(),
             reads=ottok + [("G", 22 + 2 * par), ("G", 23 + 2 * par)], writes=ottok)
        S.op("sp", (lambda r0=r0, otf=otf: lambda e: e.dma_start(out=out[r0:r0 + 128, :], in_=otf))(),
             reads=ottok, dma=True)

    def ph(name, f, *a):
        S.stage = name
        f(*a)

    def R_iter(k):
        if 0 <= k + 2 < 8:
            ph("R0(%d)" % (k + 2), R0, k + 2)
            ph("R1(%d)" % (k + 2), R1, k + 2)
            ph("R1b(%d)" % (k + 2), R1b, k + 2)
        if k % 2 == 0:
            pair = [cc for cc in (k - 1, k) if 0 <= cc < 8]
            for cc in pair:
                ph("R3a(%d)" % cc, R3a, cc)
            for cc in pair:
                ph("R3b(%d)" % cc, R3b, cc)
        if 0 <= k + 1 < 8:
            ph("R2(%d)" % (k + 1), R2, k + 1)

    def O_stage(ch):
        ph("Oa(0)", O_a, ch, 0)
        ph("Oa(1)", O_a, ch, 1)
        ph("Old", O_ld, ch, 0)
        ph("Old", O_ld, ch, 1)
        ph("Ob1(0)", O_b1, ch, 0)
        ph("Ob1(1)", O_b1, ch, 1)
        ph("Oa(2)", O_a, ch, 2)
        ph("Oa(3)", O_a, ch, 3)
        ph("Ob2(0)", O_b2, ch, 0)
        ph("Ob2(1)", O_b2, ch, 1)
        ph("Old", O_ld, ch, 2)
        ph("Old", O_ld, ch, 3)
        ph("Ob1(2)", O_b1, ch, 2)
        ph("Ob1(3)", O_b1, ch, 3)
        ph("Ob2(2)", O_b2, ch, 2)
        ph("Ob2(3)", O_b2, ch, 3)

    ph("Fpre", F_pre, 0)
    late_casts_1()
    late_casts_2()
    ph("Ftr", F_tr, 0)
    for ch in range(nch):
        first = ch == 0
        CH["p"] = ch % 2
        for k in range(-2, 9):
            R_iter(k)
            if ch > 0:
                for kind, m in M_SCHED[k]:
                    if kind == "A":
                        ph("Mb(%d)" % m, M_b, m, (ch - 1) % 2)
                    else:
                        ph("M%s(%d)" % (kind, m), M_a, m, (ch - 1) % 2, kind)
        po = ch - 1
        if ch > 0:
            ph("Oa(0)", O_a, po, 0)
            ph("Oa(1)", O_a, po, 1)
            ph("Old", O_ld, po, 0)
            ph("Old", O_ld, po, 1)
        ph("AP(0)", A_P, 0)
        if ch > 0:
            ph("Ob1(0)", O_b1, po, 0)
            ph("Ob1(1)", O_b1, po, 1)
            ph("Oa(2)", O_a, po, 2)
            ph("Oa(3)", O_a, po, 3)
        ph("AS(0)", A_S, 0, first)
        if ch > 0:
            ph("Ob2(0)", O_b2, po, 0)
            ph("Ob2(1)", O_b2, po, 1)
            ph("Old", O_ld, po, 2)
            ph("Old", O_ld, po, 3)
            ph("Ob1(2)", O_b1, po, 2)
            ph("Ob1(3)", O_b1, po, 3)
        for g in range(4):
            if g + 1 < 4:
                ph("AP(%d)" % (g + 1), A_P, g + 1)
            if g == 0 and ch > 0:
                ph("Ob2(2)", O_b2, po, 2)
                ph("Ob2(3)", O_b2, po, 3)
            if g == 3 and ch + 1 < nch:
                ph("Ftr", F_tr, ch + 1)
            ph("AV(%d)" % g, A_V, g)
            if g + 1 < 4:
                ph("AS(%d)" % (g + 1), A_S, g + 1, first)
            ph("AN(%d)" % g, A_N, g)
            if g < 3 and ch + 1 < nch:
                ph("Fpre%d" % g, F_pre, ch + 1, (g,))
    for m in range(8):
        ph("Ma(%d)" % m, M_a, m, (nch - 1) % 2)
        ph("Mb(%d)" % m, M_b, m, (nch - 1) % 2)
    O_stage(nch - 1)

    stores = [o.idx for o in S.ops if o.dma]
    fin = S.op("sp", lambda e: e.nop())
    fin.deps.update(stores)
    S.emit()
    return nc


def _layouts(pre_norm_g, w_in, b_gate, conv_w, conv_b, w_rg_a, b_rg_a, w_rg_x, b_rg_x,
             lru_lambda, attn_sinks, w_rnn_out, w_attn_out, w_out, post_norm_g):
    f = np.float32
    W = np.ascontiguousarray(w_in[0], dtype=f)
    Wd = W.reshape(8, 128, 6656)
    o_rx, o_rg, o_q, o_k, o_v, o_ag, o_ml = 0, 1024, 2048, 3072, 3328, 3584, 4608

    def cols(c0, n):
        return Wd[:, :, c0:c0 + n].transpose(1, 0, 2)

    wR = np.stack([np.concatenate([cols(o_rx + c * 128, 128), cols(o_rg + c * 128, 128)], axis=2)
                   for c in range(8)]).astype(f)
    wA = np.stack([np.concatenate([cols(o_q + g * 256, 256), cols(o_k + g * 64, 64), cols(o_k + g * 64, 64),
                                   cols(o_v + g * 64, 64), cols(o_ag + g * 256, 256)], axis=2)
                   for g in range(4)]).astype(f)
    wM = np.stack([np.concatenate([cols(o_ml + m * 128, 128), cols(o_ml + 1024 + m * 128, 128)], axis=2)
                   for m in range(8)]).astype(f)

    def rows(w):
        return np.ascontiguousarray(w[0].reshape(8, 128, 1024).transpose(1, 0, 2), dtype=f)
    wO3 = np.stack([rows(w_rnn_out), rows(w_attn_out), rows(w_out)])
    wG = np.zeros((2, 128, 8, 128), f)
    for i, wg in enumerate((w_rg_a[0], w_rg_x[0])):
        for c in range(8):
            wG[i, 0:64, c, 0:64] = wg[2 * c]
            wG[i, 64:128, c, 64:128] = wg[2 * c + 1]
    pvec = np.zeros((128, PV_N), f)
    pvec[:, PV_CW:PV_CW + 32] = conv_w[0].reshape(4, 8, 128).transpose(2, 1, 0).reshape(128, 32)
    pvec[:, PV_CB:PV_CB + 8] = conv_b[0].reshape(8, 128).T
    pvec[:, PV_BA:PV_BA + 8] = b_rg_a[0].reshape(8, 128).T
    pvec[:, PV_BX:PV_BX + 8] = b_rg_x[0].reshape(8, 128).T
    pvec[:, PV_LAM:PV_LAM + 8] = lru_lambda[0].reshape(8, 128).T
    sk = attn_sinks[0].reshape(8, 2)
    pvec[0:64, PV_SINK:PV_SINK + 8] = sk[:, 0][None, :]
    pvec[64:128, PV_SINK:PV_SINK + 8] = sk[:, 1][None, :]
    pvec[:, PV_BG:PV_BG + 16] = b_gate[0].reshape(16, 128).T
    gbc = np.stack([np.broadcast_to(pre_norm_g[0][None, :], (128, 1024)),
                    np.broadcast_to(post_norm_g[0][None, :], (128, 1024))]).astype(f)
    kk = np.arange(128)[:, None]
    qq = np.arange(128)[None, :]
    dm = np.empty((128, 256), f)
    dm[:, 0:128] = np.where(qq >= kk, -(qq - kk), -1.0e5)
    dm[:, 128:256] = np.where(qq < kk, -(128 + qq - kk), -1.0e5)
    return dict(wR=wR, wA=wA, wM=wM, wO3=wO3, wG=wG, pvec=pvec, gbc=np.ascontiguousarray(gbc), dmask=dm)


_NC_CACHE = {}


def kernel(x, pre_norm_g, w_in, b_gate, conv_w, conv_b, w_rg_a, b_rg_a, w_rg_x, b_rg_x,
           lru_lambda, attn_sinks, w_rnn_out, w_attn_out, w_out, post_norm_g):
    x = np.asarray(x, dtype=np.float32)
    shared = _layouts(*[np.asarray(a, dtype=np.float32) for a in (
        pre_norm_g, w_in, b_gate, conv_w, conv_b, w_rg_a, b_rg_a, w_rg_x, b_rg_x,
        lru_lambda, attn_sinks, w_rnn_out, w_attn_out, w_out, post_norm_g)])
    nc = build_program()
    in_maps = []
    for b in range(N_CORES):
        m = dict(shared)
        m["x"] = np.ascontiguousarray(x[b])
        in_maps.append(m)
    res = run_bass_kernel_spmd(nc, in_maps, core_ids=list(range(N_CORES)))
    return np.stack([np.asarray(r["out"], dtype=np.float32) for r in res.results], axis=0)
```
